# Optimizing a Trainium2 kernel written in Bass

```python
import math
import jax, jax.numpy as jnp
from jax import lax
import numpy as np

D_MODEL = 2048
BATCH = 4
SEQ = 2048
DEPTH = 2
DEC_BATCH = 32
DEC_SEQ = 64
PAST_LEN = 4096

CHUNK = 64
N_A_LAYERS = DEPTH // 2
N_B_LAYERS = DEPTH - N_A_LAYERS
HEAD_DIM = 128
D_MIX = D_MODEL
D_MEM = D_MIX // 4
N_MEM_HEADS = D_MEM // HEAD_DIM
D_MAIN = D_MIX - D_MEM
N_SB_HEADS = D_MAIN // HEAD_DIM
CONV_W = 3
N_MEM = 256
D_FF = 4 * D_MODEL
Q_BLOCK = 128
K_BLOCK = 128
ALPHA = (2.0 * DEPTH) ** 0.25
BETA = (8.0 * DEPTH) ** -0.25
LN_EPS = 1e-5

kernel_name = 'yoco_shortconv_stickbreak_step'


def layer_norm(x, g, b):
    xf = x.astype(jnp.float32)
    mu = jnp.mean(xf, axis=-1, keepdims=True)
    var = jnp.mean(jnp.square(xf - mu), axis=-1, keepdims=True)
    y = (xf - mu) * lax.rsqrt(var + LN_EPS) * g.astype(jnp.float32) + b.astype(jnp.float32)
    return y.astype(x.dtype)


def short_conv(u, w, prev):
    T = u.shape[1]
    up = jnp.concatenate([prev, u], axis=1)
    y = up[:, 0:T] * w[:, 0]
    for j in range(1, CONV_W):
        y = y + up[:, j:j + T] * w[:, j]
    return y, up[:, -(CONV_W - 1):]


def mem_attend(q, mk, mv):
    s = jnp.einsum('bthd,bmhd->bhtm', q, mk).astype(jnp.float32) / math.sqrt(HEAD_DIM)
    p = jax.nn.softmax(s, axis=-1).astype(mv.dtype)
    return jnp.einsum('bhtm,bmhd->bthd', p, mv)


def stick_breaking(q, k, v, q_start):
    B, Tq, H, Dh = q.shape
    Tk = k.shape[1]
    qb = Q_BLOCK if Tq % Q_BLOCK == 0 else Tq
    nq = Tq // qb
    nk = -(-Tk // K_BLOCK)
    pad = nk * K_BLOCK - Tk
    k = jnp.pad(k, ((0, 0), (0, pad), (0, 0), (0, 0)))
    v = jnp.pad(v, ((0, 0), (0, pad), (0, 0), (0, 0)))
    kb = k.reshape(B, nk, K_BLOCK, H, Dh).transpose(1, 0, 2, 3, 4)
    vb = v.reshape(B, nk, K_BLOCK, H, Dh).transpose(1, 0, 2, 3, 4)
    k_pos = jnp.arange(nk * K_BLOCK, dtype=jnp.int32).reshape(nk, K_BLOCK)
    qs = q.reshape(B, nq, qb, H, Dh).transpose(1, 0, 2, 3, 4)
    q_pos = (q_start + jnp.arange(Tq, dtype=jnp.int32)).reshape(nq, qb)
    scale = 1.0 / math.sqrt(Dh)

    def q_block(args):
        qi, qp = args

        def k_step(carry, kargs):
            acc, out = carry
            ki, vi, kp = kargs
            z = jnp.einsum('bqhd,bkhd->bhqk', qi, ki).astype(jnp.float32) * scale
            valid = kp[None, :] < qp[:, None]
            u = jnp.where(valid, jax.nn.log_sigmoid(-z), 0.0)
            suffix = lax.cumsum(u, axis=3, reverse=True) - u
            log_a = jax.nn.log_sigmoid(z) + suffix + acc[..., None]
            a = jnp.where(valid, jnp.exp(log_a), 0.0)
            out = out + jnp.einsum('bhqk,bkhd->bhqd', a, vi.astype(jnp.float32))
            return (acc + jnp.sum(u, axis=-1), out), None

        init = (jnp.zeros((B, H, qb), jnp.float32), jnp.zeros((B, H, qb, Dh), jnp.float32))
        (_, out), _ = lax.scan(k_step, init, (kb, vb, k_pos), reverse=True)
        return out.transpose(0, 2, 1, 3)

    o = lax.map(q_block, (qs, q_pos))
    return o.transpose(1, 0, 2, 3, 4).reshape(B, Tq, H, Dh).astype(q.dtype)


def mem_kv_from_tokens(mem, w_mem_kv):
    kv = jnp.einsum('bmd,lde->lbme', mem, w_mem_kv)
    B = mem.shape[0]
    mk = kv[..., :D_MEM].reshape(DEPTH, B, N_MEM, N_MEM_HEADS, HEAD_DIM)
    mv = kv[..., D_MEM:].reshape(DEPTH, B, N_MEM, N_MEM_HEADS, HEAD_DIM)
    return mk, mv


def trunk(x, conv_prev, k_past, v_past, mem_k, mem_v, q_start,
          w_in_a, conv_w, w_in_b, w_kv, w_out, w_ff1, w_ff2, ln_g, ln_b):
    B, T, _ = x.shape
    conv_states = []
    k_new = None
    v_new = None
    k_all = None
    v_all = None
    for layer in range(DEPTH):
        if layer < N_A_LAYERS:
            h = x @ w_in_a[layer]
            xin = h[..., :D_MAIN]
            gate_b = h[..., D_MAIN:2 * D_MAIN]
            gate_c = h[..., 2 * D_MAIN:3 * D_MAIN]
            q_m = h[..., 3 * D_MAIN:]
            cu, st = short_conv(gate_c * xin, conv_w[layer], conv_prev[layer])
            conv_states.append(st)
            main = gate_b * cu
        else:
            if k_new is None:
                kv = x @ w_kv
                k_new = kv[..., :D_MAIN].reshape(B, T, N_SB_HEADS, HEAD_DIM)
                v_new = kv[..., D_MAIN:].reshape(B, T, N_SB_HEADS, HEAD_DIM)
                if k_past is None:
                    k_all, v_all = k_new, v_new
                else:
                    k_all = jnp.concatenate([k_past, k_new], axis=1)
                    v_all = jnp.concatenate([v_past, v_new], axis=1)
            h = x @ w_in_b[layer - N_A_LAYERS]
            q_sb = h[..., :D_MAIN].reshape(B, T, N_SB_HEADS, HEAD_DIM)
            q_m = h[..., D_MAIN:]
            main = stick_breaking(q_sb, k_all, v_all, q_start).reshape(B, T, D_MAIN)
        mo = mem_attend(q_m.reshape(B, T, N_MEM_HEADS, HEAD_DIM), mem_k[layer], mem_v[layer])
        mix = jnp.concatenate([main, mo.reshape(B, T, D_MEM)], axis=-1) @ w_out[layer]
        x = layer_norm(ALPHA * x + mix, ln_g[layer, 0], ln_b[layer, 0])
        f = jnp.square(jax.nn.relu(x @ w_ff1[layer])) @ w_ff2[layer]
        x = layer_norm(ALPHA * x + f, ln_g[layer, 1], ln_b[layer, 1])
    return x, jnp.stack(conv_states), k_new, v_new


def setup_inputs(seed: int = 0) -> dict:
    key = jax.random.key(seed)
    ks = jax.random.split(key, 24)

    def nrm(k, shape, s=1.0):
        return jax.random.normal(k, shape, jnp.float32) * s

    d_in_a = 3 * D_MAIN + D_MEM
    d_in_b = D_MAIN + D_MEM
    return {
        'x_prompt': nrm(ks[0], (BATCH, SEQ, D_MODEL)),
        'x_sample': nrm(ks[1], (DEC_BATCH, DEC_SEQ, D_MODEL)),
        'state_conv': nrm(ks[2], (N_A_LAYERS, DEC_BATCH, CONV_W - 1, D_MAIN)),
        'cache_k': nrm(ks[3], (DEC_BATCH, PAST_LEN, N_SB_HEADS, HEAD_DIM)),
        'cache_v': nrm(ks[4], (DEC_BATCH, PAST_LEN, N_SB_HEADS, HEAD_DIM)),
        'cache_mem_k': nrm(ks[5], (DEPTH, DEC_BATCH, N_MEM, N_MEM_HEADS, HEAD_DIM)),
        'cache_mem_v': nrm(ks[6], (DEPTH, DEC_BATCH, N_MEM, N_MEM_HEADS, HEAD_DIM)),
        'mem_prompt': nrm(ks[7], (BATCH, N_MEM, D_MODEL)),
        'w_in_a': nrm(ks[8], (N_A_LAYERS, D_MODEL, d_in_a), D_MODEL ** -0.5),
        'conv_w': nrm(ks[9], (N_A_LAYERS, D_MAIN, CONV_W), CONV_W ** -0.5),
        'w_in_b': nrm(ks[10], (N_B_LAYERS, D_MODEL, d_in_b), D_MODEL ** -0.5),
        'w_kv': jnp.concatenate([nrm(ks[11], (D_MODEL, D_MAIN), D_MODEL ** -0.5),
                                 nrm(ks[12], (D_MODEL, D_MAIN), D_MODEL ** -0.5 * BETA)], axis=-1),
        'w_mem_kv': jnp.concatenate([nrm(ks[13], (DEPTH, D_MODEL, D_MEM), D_MODEL ** -0.5),
                                     nrm(ks[14], (DEPTH, D_MODEL, D_MEM), D_MODEL ** -0.5 * BETA)], axis=-1),
        'w_out': nrm(ks[15], (DEPTH, D_MIX, D_MODEL), D_MIX ** -0.5 * BETA),
        'w_ff1': nrm(ks[16], (DEPTH, D_MODEL, D_FF), D_MODEL ** -0.5),
        'w_ff2': nrm(ks[17], (DEPTH, D_FF, D_MODEL), D_FF ** -0.5 * BETA),
        'ln_g': 1.0 + nrm(ks[18], (DEPTH, 2, D_MODEL), 0.01),
        'ln_b': nrm(ks[19], (DEPTH, 2, D_MODEL), 0.01),
    }


def reference(x_prompt, x_sample, state_conv, cache_k, cache_v, cache_mem_k, cache_mem_v,
              mem_prompt, w_in_a, conv_w, w_in_b, w_kv, w_mem_kv, w_out, w_ff1, w_ff2,
              ln_g, ln_b):
    mem_k_p, mem_v_p = mem_kv_from_tokens(mem_prompt, w_mem_kv)
    conv_zero = jnp.zeros((N_A_LAYERS, x_prompt.shape[0], CONV_W - 1, D_MAIN), x_prompt.dtype)
    y_prompt, conv_p, k_p, v_p = trunk(
        x_prompt, conv_zero, None, None, mem_k_p, mem_v_p, 0,
        w_in_a, conv_w, w_in_b, w_kv, w_out, w_ff1, w_ff2, ln_g, ln_b)
    y_sample, conv_s, k_s, v_s = trunk(
        x_sample, state_conv, cache_k, cache_v, cache_mem_k, cache_mem_v, cache_k.shape[1],
        w_in_a, conv_w, w_in_b, w_kv, w_out, w_ff1, w_ff2, ln_g, ln_b)
    return (y_prompt, y_sample, conv_p, conv_s, k_p, v_p, k_s, v_s, mem_k_p, mem_v_p)
```

```python
import os
import numpy as np
from contextlib import ExitStack
import concourse.bass as bass
import concourse.mybir as mybir
from concourse.bass_utils import run_bass_kernel_spmd

F32 = mybir.dt.float32
BF16 = mybir.dt.bfloat16
AF = mybir.ActivationFunctionType
ALU = mybir.AluOpType
AX = mybir.AxisListType

D = 2048
DMAIN = 1536
DMEM = 512
NMEM = 256
DFF = 8192
ALPHA = 4.0 ** 0.25
EPS = 1e-5
SCALE = 128.0 ** -0.5
NT = 10
NTOK = NT * 128
NCORES = 8
DEBUG = bool(os.environ.get("KDBG"))


class Buf:
    __slots__ = ("w", "r")

    def __init__(self):
        self.w = None
        self.r = {}


class PBank:
    def __init__(self, ap, buf):
        self.ap = ap
        self.buf = buf


class Trk:
    def __init__(self, nc, es):
        self.nc = nc
        self.eng = {"pe": nc.tensor, "act": nc.scalar, "dve": nc.vector, "pool": nc.gpsimd, "sp": nc.sync}
        self.sem = {k: es.enter_context(nc.semaphore("s_" + k)) for k in self.eng}
        self.cnt = {k: 0 for k in self.eng}
        self.waited = {k: {} for k in self.eng}
        self.dsems = {}
        self.didx = {}
        self.dval = {}
        for q, n in (("sp", 20), ("pool", 24)):
            self.dsems[q] = [es.enter_context(nc.semaphore("d_%s%d" % (q, i))) for i in range(n)]
            self.didx[q] = 0
            for i in range(n):
                self.dval[(q, i)] = 0
        self.last = {}
        self.dma_pending = {}

    def _wait(self, e, tok):
        if tok is None:
            return
        s, v, key = tok
        if self.waited[e].get(key, 0) >= v:
            return
        self.eng[e].wait_ge(s, v)
        self.waited[e][key] = v

    def _deps(self, e, reads, writes):
        for b in reads:
            self._wait(e, b.w)
        for b in writes:
            self._wait(e, b.w)
            for t in b.r.values():
                self._wait(e, t)

    def _commit(self, tok, reads, writes):
        for b in reads:
            b.r[tok[2]] = tok
        for b in writes:
            b.w = tok
            b.r = {}

    def op(self, e, fn, reads=(), writes=()):
        self._deps(e, reads, writes)
        inst = fn(self.eng[e])
        self.cnt[e] += 1
        inst.then_inc(self.sem[e], 1)
        tok = (self.sem[e], self.cnt[e], e)
        self.last[e] = tok
        self._commit(tok, reads, writes)
        return tok

    def group(self, e, fns, reads=(), writes=()):
        self._deps(e, reads, writes)
        inst = None
        for fn in fns:
            inst = fn(self.eng[e])
        self.cnt[e] += 1
        inst.then_inc(self.sem[e], 1)
        tok = (self.sem[e], self.cnt[e], e)
        self.last[e] = tok
        self._commit(tok, reads, writes)
        return tok

    def dma(self, q, out, in_, reads=(), writes=(), **kw):
        self._deps(q, reads, writes)
        i = self.didx[q]
        self.didx[q] = (i + 1) % len(self.dsems[q])
        s = self.dsems[q][i]
        key = (q, i)
        prev = self.dval[key]
        if prev:
            self._wait(q, (s, prev, key))
        self.eng[q].dma_start(out=out, in_=in_, **kw).then_inc(s, 16)
        self.dval[key] = prev + 16
        tok = (s, prev + 16, key)
        self.dma_pending[key] = tok
        self._commit(tok, reads, writes)
        return tok

    def barrier(self):
        toks = list(self.last.values()) + list(self.dma_pending.values())
        for e in self.eng:
            for t in toks:
                self._wait(e, t)
        self.dma_pending = {}


class WStream:
    NSLOT = 6

    def __init__(self, T, slots):
        self.T = T
        self.slots = slots
        self.NSLOT = len(slots)
        self.queue = []
        self.issued = 0
        self.taken = 0
        self.done = 0

    def plan(self, pieces):
        self.queue += pieces

    def release(self, n):
        self.done += n

    def _pump(self):
        lim = min(len(self.queue), self.done + self.NSLOT)
        while self.issued < lim:
            p = self.queue[self.issued]
            st, sb = self.slots[self.issued % self.NSLOT]
            for view, src in p:
                self.T.dma("pool", out=view(st), in_=src, writes=[sb])
            self.issued += 1

    def take(self):
        self._pump()
        st, sb = self.slots[self.taken % self.NSLOT]
        self.taken += 1
        return st, sb


def build_program():
    nc = bass.Bass("TRN2", target_bir_lowering=False)

    def din(name, shape):
        return nc.dram_tensor(name, list(shape), F32, kind="ExternalInput").ap()

    def dout(name, shape):
        return nc.dram_tensor(name, list(shape), F32, kind="ExternalOutput").ap()

    x_own = din("x_own", [8, 128, D])
    x_oth = din("x_oth", [8, 128, D])
    x_smp = din("x_smp", [2, 128, D])
    h_own = din("h_own", [16, D])
    h_oth = din("h_oth", [16, D])
    sconv = din("sconv", [8, DMAIN])
    cache_k = din("cache_k", [4, 4096, DMAIN])
    cache_v = din("cache_v", [4, 4096, DMAIN])
    cmk = din("cmk", [2, 4, NMEM, DMEM])
    cmv = din("cmv", [2, 4, NMEM, DMEM])
    mem_p = din("mem_p", [NMEM, D])
    w_in_a = din("w_in_a", [D, 5120])
    conv_w = din("conv_w", [DMAIN, 3])
    w_in_b = din("w_in_b", [D, D])
    w_kv = din("w_kv", [D, 3072])
    w_mem_kv = din("w_mem_kv", [2, D, 1024])
    w_out = din("w_out", [2, D, D])
    w_ff1 = din("w_ff1", [2, D, DFF])
    w_ff2 = din("w_ff2", [2, DFF, D])
    ln_g = din("ln_g", [4, D])
    ln_b = din("ln_b", [4, D])
    c_ident = din("c_ident", [128, 128])
    c_tri = din("c_tri", [128, 128])
    c_ones = din("c_ones", [128, 128])
    c_mtri = din("c_mtri", [128, 128])
    c_m0 = din("c_m0", [128, 128])

    y_out = dout("y_out", [NT, 128, D])
    kv_own = dout("kv_own", [8, 128, 3072])
    kv_oth = dout("kv_oth", [8, 128, 3072])
    kv_smp = dout("kv_smp", [2, 128, 3072])
    conv_out = dout("conv_out", [10, DMAIN])
    memkv_p = dout("memkv_p", [2, NMEM, 1024])

    with ExitStack() as es:
        T = Trk(nc, es)

        uid = {"n": 0}

        def sb(name, shape, dt, stack=es):
            uid["n"] += 1
            return stack.enter_context(nc.sbuf_tensor("%s_%d" % (name, uid["n"]), list(shape), dt))

        resid = sb("resid", [128, NT, D], F32)
        xT = sb("xT", [128, 16, NTOK], BF16)
        QX = sb("QX", [128, 16, NTOK], BF16)
        B_res = [Buf() for _ in range(NT)]
        B_xT = [Buf() for _ in range(NT)]
        B_qx = [Buf() for _ in range(NT)]
        identb = sb("identb", [128, 128], BF16)
        tri = sb("tri", [128, 128], BF16)
        ones = sb("ones", [128, 128], BF16)
        mtri = sb("mtri", [128, 128], F32)
        m0 = sb("m0", [128, 128], F32)
        cw = sb("cw", [128, 12, 3], F32)
        xTh = sb("xTh", [128, 16, 16], BF16)
        small = sb("small", [128, 64], F32)
        B_const = Buf()
        B_xTh = Buf()
        B_small = Buf()

        pbanks = []
        for i in range(6):
            t = es.enter_context(nc.psum_tensor("pb%d" % i, [128, 512], F32))
            pbanks.append(PBank(t, Buf()))
        tbanks = []
        for i in range(2):
            t = es.enter_context(nc.psum_tensor("tb%d" % i, [128, 1024], BF16))
            tbanks.append(PBank(t, Buf()))
        rr = {"p": 0, "t": 0, "e": 0}

        def psum(avoid=()):
            while True:
                b = pbanks[rr["p"] % 6]
                rr["p"] += 1
                if b not in avoid:
                    return b

        def tpsum():
            b = tbanks[rr["t"] % 2]
            rr["t"] += 1
            return b

        def evac_eng():
            rr["e"] += 1
            return "act" if rr["e"] % 2 else "dve"

        def copy_op(e, out, in_, reads, writes):
            if e == "act":
                return T.op("act", lambda a: a.activation(out=out, in_=in_, func=AF.Copy), reads, writes)
            return T.op("dve", lambda v: v.tensor_copy(out=out, in_=in_), reads, writes)

        with ExitStack() as ph:
            stg = sb("cstg", [128, 128], F32, ph)
            bs = Buf()
            T.dma("sp", out=stg[:], in_=c_ident, writes=[bs])
            T.op("dve", lambda v: v.tensor_copy(out=identb[:], in_=stg[:]), [bs], [B_const])
            stg2 = sb("cstg2", [128, 128], F32, ph)
            stg3 = sb("cstg3", [128, 128], F32, ph)
            bs2, bs3 = Buf(), Buf()
            T.dma("sp", out=stg2[:], in_=c_tri, writes=[bs2])
            T.op("dve", lambda v: v.tensor_copy(out=tri[:], in_=stg2[:]), [bs2], [B_const])
            T.dma("sp", out=stg3[:], in_=c_ones, writes=[bs3])
            T.op("dve", lambda v: v.tensor_copy(out=ones[:], in_=stg3[:]), [bs3], [B_const])
            T.dma("sp", out=mtri[:], in_=c_mtri, writes=[B_const])
            T.dma("sp", out=m0[:], in_=c_m0, writes=[B_const])
            T.dma("sp", out=cw[:], in_=conv_w.rearrange("(j p) k -> p j k", p=128), writes=[B_const],
                  allow_slow_non_contiguous=True)
            T.barrier()

        def make_xT_tile(ph_xbf, src_ap, src_bufs, dstT, dst_col, dst_bufs, nrows=128, ncolT=128):
            xbf, bx = ph_xbf[rr["e"] % 2]
            rr["e"] += 1
            T.op("act", lambda a: a.activation(out=xbf[0:nrows, :], in_=src_ap, func=AF.Copy), src_bufs, [bx])
            for half in range(2):
                tb = tpsum()
                T.group("pe", [
                    (lambda pe, kc=kc: pe.transpose(out=tb.ap[:, (kc % 8) * 128:(kc % 8) * 128 + nrows],
                                                    in_=xbf[0:nrows, kc * 128:(kc + 1) * 128],
                                                    identity=identb[0:nrows, 0:nrows]))
                    for kc in range(half * 8, half * 8 + 8)], [bx, B_const], [tb.buf])
                src3 = tb.ap[:, :].rearrange("p (a b) -> p a b", b=128)[:, :, 0:nrows]
                copy_op(evac_eng(), dstT[:, half * 8:half * 8 + 8, dst_col:dst_col + nrows], src3, [tb.buf], dst_bufs)

        def fm_pieces(W, cols, kc_n=16):
            pieces = []
            for col in cols:
                pieces.append([(lambda st, kc_n=kc_n: st[:, 0:kc_n * 128].rearrange("p (k c) -> p k c", c=128),
                                W[:, col:col + 128].rearrange("(k p) c -> p k c", p=128))])
            return pieces

        def linear_fm(ws, src, src_bufs_of, W, cols, groups, consume, kc_n=16, preplanned=False):
            if not preplanned:
                ws.plan(fm_pieces(W, cols, kc_n))
            for ci, col in enumerate(cols):
                st, sbuf = ws.take()
                wv = st[:, 0:kc_n * 128].rearrange("p (k c) -> p k c", c=128)
                for gi, (c0, n, s_ap) in enumerate(groups):
                    pb = psum()
                    sap = src if s_ap is None else s_ap
                    T.group("pe", [
                        (lambda pe, kc=kc: pe.matmul(pb.ap[:, 0:n], lhsT=wv[:, kc, :], rhs=sap[:, kc, c0:c0 + n],
                                                     start=(kc == 0), stop=(kc == kc_n - 1)))
                        for kc in range(kc_n)], [sbuf] + src_bufs_of(gi), [pb.buf])
                    consume(ci, gi, pb)
                ws.release(1)

        def tm_pieces(W, kc_n, cblocks):
            npc = (kc_n + 7) // 8
            pieces = []
            for col in cblocks:
                for pi in range(npc):
                    k0 = pi * 8
                    kn = min(8, kc_n - k0)
                    pieces.append([(lambda st, kn=kn: st[:, 0:kn * 256].rearrange("p (k c) -> p k c", c=256),
                                    W[k0 * 128:(k0 + kn) * 128, col:col + 256].rearrange("(k p) c -> p k c", p=128))])
            return pieces

        def linear_tm(ws, src, W, kc_n, cblocks, segs, consume, preplanned=False):
            npc = (kc_n + 7) // 8
            if not preplanned:
                ws.plan(tm_pieces(W, kc_n, cblocks))
            for bi, col in enumerate(cblocks):
                sl = [ws.take() for _ in range(npc)]
                for si, (c0, nt, sbufs) in enumerate(segs):
                    pb = psum()
                    fns = []
                    for kc in range(kc_n):
                        st = sl[kc // 8][0]
                        wv = st[:, 0:2048].rearrange("p (k c) -> p k c", c=256)
                        fns.append(lambda pe, kc=kc, wv=wv: pe.matmul(pb.ap[0:nt, 0:256], lhsT=src[:, kc, c0:c0 + nt],
                                                                      rhs=wv[:, kc % 8, :], start=(kc == 0),
                                                                      stop=(kc == kc_n - 1)))
                    T.group("pe", fns, [s[1] for s in sl] + list(sbufs), [pb.buf])
                    consume(bi, si, pb)
                ws.release(npc)

        def make_ws(ph, nslot=6):
            slots = []
            for i in range(nslot):
                slots.append((sb("wslot%d" % i, [128, 2048], BF16, ph), Buf()))
            return WStream(T, slots)

        def ln_tiles(idx, tiles, scale_after, ph, nxb=2):
            gt = sb("ln_gt", [128, D], F32, ph)
            bt = sb("ln_bt", [128, D], F32, ph)
            xb1 = [(sb("ln_xbf%d" % i, [128, D], BF16, ph), Buf()) for i in range(nxb)]
            xb = [xb1[i % nxb] for i in range(2)]
            lnsm = sb("ln_sm", [128, NT, 8], F32, ph)
            bnst = [(sb("ln_bnst%d" % i, [128, 4, 6], F32, ph), Buf()) for i in range(2)]
            bg, bsm = Buf(), Buf()
            T.dma("sp", out=gt[:], in_=ln_g[idx].partition_broadcast(128), writes=[bg])
            T.dma("sp", out=bt[:], in_=ln_b[idx].partition_broadcast(128), writes=[bg])
            nt_ = len(tiles)
            t0_ = tiles[0]
            for k, ti in enumerate(tiles):
                bs, bb = bnst[k % 2]
                for q in range(4):
                    T.op("dve", lambda v, q=q: v.bn_stats(out=bs[:, q, :], in_=resid[:, ti, q * 512:(q + 1) * 512]),
                         [B_res[ti]], [bb])
                T.op("dve", lambda v: v.bn_aggr(out=lnsm[:, ti, 0:2], in_=bs[:].rearrange("p a b -> p (a b)")), [bb], [bsm])
            sl = lnsm[:, t0_:t0_ + nt_, :]
            T.op("dve", lambda v: v.tensor_scalar(out=sl[:, :, 2:3], in0=sl[:, :, 1:2], scalar1=EPS, scalar2=None,
                                                  op0=ALU.add), [bsm], [bsm])
            T.op("act", lambda a: a.activation(out=sl[:, :, 3:4], in_=sl[:, :, 2:3], func=AF.Sqrt), [bsm], [bsm])
            T.op("dve", lambda v: v.reciprocal(out=sl[:, :, 4:5], in_=sl[:, :, 3:4]), [bsm], [bsm])
            T.op("dve", lambda v: v.scalar_tensor_tensor(out=sl[:, :, 5:6], in0=sl[:, :, 0:1], scalar=-1.0, in1=sl[:, :, 4:5],
                                                         op0=ALU.mult, op1=ALU.mult), [bsm], [bsm])
            def stage1(ti):
                r = resid[:, ti, :]
                T.op("act", lambda a: a.activation(out=r, in_=r, func=AF.Identity, bias=lnsm[:, ti, 5:6],
                                                   scale=lnsm[:, ti, 4:5]), [bsm, B_res[ti]], [B_res[ti]])
                T.op("dve", lambda v: v.tensor_tensor(out=r, in0=r, in1=gt[:], op=ALU.mult), [bg, B_res[ti]], [B_res[ti]])
                T.op("pool", lambda g: g.tensor_tensor(out=r, in0=r, in1=bt[:], op=ALU.add), [bg, B_res[ti]], [B_res[ti]])

            def stage2(ti):
                r = resid[:, ti, :]
                make_xT_tile(xb, r, [B_res[ti]], xT, ti * 128, [B_xT[ti]])
                if scale_after:
                    T.op("act", lambda a: a.activation(out=r, in_=r, func=AF.Copy, scale=ALPHA), [B_res[ti]], [B_res[ti]])

            for k in range(nt_ + 2):
                if k < nt_:
                    stage1(tiles[k])
                if 0 <= k - 2 < nt_:
                    stage2(tiles[k - 2])

        def layer_norm_phase(idx, tiles, scale_after):
            with ExitStack() as ph:
                ln_tiles(idx, tiles, scale_after, ph)
                T.barrier()

        def mem_attention(layer, segs):
            with ExitStack() as ph:
                mkb = [(sb("mkb%d" % i, [128, 2, 512], BF16, ph), Buf()) for i in range(2)]
                mvb = [(sb("mvb%d" % i, [128, 2, 512], BF16, ph), Buf()) for i in range(2)]
                mkT = [(sb("mkT%d" % i, [128, 4, 256], BF16, ph), Buf()) for i in range(2)]
                Pf = [(sb("Pf%d" % i, [128, 256], F32, ph), Buf()) for i in range(2)]
                Pb = [(sb("Pb%d" % i, [128, 256], BF16, ph), Buf()) for i in range(2)]
                PT = [(sb("PT%d" % i, [128, 2, 128], BF16, ph), Buf()) for i in range(2)]
                sm = sb("ma_small", [128, 16], F32, ph)
                bsm = Buf()
                def load_set(si, mk_d, mv_d):
                    T.dma("pool", out=mkb[si][0][:], in_=mk_d.rearrange("(m p) c -> p m c", p=128), writes=[mkb[si][1]])
                    T.dma("pool", out=mvb[si][0][:], in_=mv_d.rearrange("(m p) c -> p m c", p=128), writes=[mvb[si][1]])
                    tb = tpsum()
                    T.group("pe", [
                        (lambda pe, m=m, mc=mc: pe.transpose(out=tb.ap[:, (m * 2 + mc) * 128:(m * 2 + mc + 1) * 128],
                                                             in_=mkb[si][0][:, mc, m * 128:(m + 1) * 128],
                                                             identity=identb[:]))
                        for m in range(4) for mc in range(2)], [mkb[si][1], B_const], [tb.buf])
                    copy_op("dve", mkT[si][0][:].rearrange("p a b -> p (a b)"), tb.ap[:, :], [tb.buf], [mkT[si][1]])

                cur = {"key": None, "i": -1}
                cnt = 0
                units = []
                setload = {}
                bsmk = [Buf(), Buf()]
                for (c0, nt, ti, setkey, mk_d, mv_d) in segs:
                    if setkey != cur["key"]:
                        cur["key"] = setkey
                        cur["i"] += 1
                        si = cur["i"] % 2

                        def do_load(si=si, mk_d=mk_d, mv_d=mv_d):
                            load_set(si, mk_d, mv_d)
                        setload[(c0, nt, ti, si, 0, cnt % 2)] = do_load
                    si = cur["i"] % 2
                    for m in range(4):
                        k = cnt % 2
                        cnt += 1
                        units.append((c0, nt, ti, si, m, k))

                def ma_stage1(c0, nt, ti, si, m, k):
                    pb = psum()
                    T.group("pe", [lambda pe: pe.matmul(pb.ap[0:nt, 0:256], lhsT=QX[:, 12 + m, c0:c0 + nt],
                                                        rhs=mkT[si][0][:, m, :], start=True, stop=True)],
                            [B_qx[ti], mkT[si][1]], [pb.buf])
                    smk = sm[:, k * 8:(k + 1) * 8]
                    T.op("dve", lambda v: v.reduce_max(out=smk[0:nt, 0:1], in_=pb.ap[0:nt, 0:256], axis=AX.X),
                         [pb.buf], [bsmk[k]])
                    T.op("dve", lambda v: v.tensor_scalar(out=smk[0:nt, 1:2], in0=smk[0:nt, 0:1], scalar1=-SCALE,
                                                          scalar2=None, op0=ALU.mult), [bsmk[k]], [bsmk[k]])
                    T.op("act", lambda a: a.activation(out=Pf[k][0][0:nt, :], in_=pb.ap[0:nt, 0:256], func=AF.Exp,
                                                       bias=smk[0:nt, 1:2], scale=SCALE, accum_out=smk[0:nt, 2:3]),
                         [pb.buf, bsmk[k]], [Pf[k][1], bsmk[k]])
                    T.op("dve", lambda v: v.reciprocal(out=smk[0:nt, 3:4], in_=smk[0:nt, 2:3]), [bsmk[k]], [bsmk[k]])
                    T.op("dve", lambda v: v.tensor_scalar(out=Pb[k][0][0:nt, :], in0=Pf[k][0][0:nt, :],
                                                          scalar1=smk[0:nt, 3:4], scalar2=None, op0=ALU.mult),
                         [bsmk[k], Pf[k][1]], [Pb[k][1]])

                def ma_stage2(c0, nt, ti, si, m, k):
                    tb = tpsum()
                    T.group("pe", [
                        (lambda pe, mc=mc: pe.transpose(out=tb.ap[:, mc * 128:mc * 128 + nt],
                                                        in_=Pb[k][0][0:nt, mc * 128:(mc + 1) * 128],
                                                        identity=identb[0:nt, 0:nt]))
                        for mc in range(2)], [Pb[k][1], B_const], [tb.buf])
                    copy_op("act", PT[k][0][:, :, 0:nt], tb.ap[:, 0:256].rearrange("p (a b) -> p a b", b=128)[:, :, 0:nt],
                            [tb.buf], [PT[k][1]])
                    pb2 = psum()
                    T.group("pe", [
                        (lambda pe, mc=mc: pe.matmul(pb2.ap[:, 0:nt], lhsT=mvb[si][0][:, mc, m * 128:(m + 1) * 128],
                                                     rhs=PT[k][0][:, mc, 0:nt], start=(mc == 0), stop=(mc == 1)))
                        for mc in range(2)], [mvb[si][1], PT[k][1]], [pb2.buf])
                    copy_op("dve", QX[:, 12 + m, c0:c0 + nt], pb2.ap[:, 0:nt], [pb2.buf], [B_qx[ti]])

                for u in range(len(units) + 1):
                    if u < len(units):
                        if units[u] in setload:
                            setload[units[u]]()
                        ma_stage1(*units[u])
                    if u >= 1:
                        ma_stage2(*units[u - 1])
                T.barrier()

        def out_ln_ffn(layer, tiles, dbg=None, do_ln2=True):
            ntok = len(tiles) * 128
            segs = [(ti * 128, 128, [B_qx[ti]]) for ti in tiles]
            with ExitStack() as ph:
                ws = make_ws(ph)

                def cons(bi, si, pb):
                    ti = tiles[si]
                    rv = resid[:, ti, bi * 256:(bi + 1) * 256]
                    T.op("dve", lambda v: v.scalar_tensor_tensor(out=rv, in0=rv, scalar=ALPHA, in1=pb.ap[:, 0:256],
                                                                 op0=ALU.mult, op1=ALU.add), [pb.buf, B_res[ti]], [B_res[ti]])
                linear_tm(ws, QX, w_out[layer], 16, [i * 256 for i in range(8)], segs, cons)
                T.barrier()
            with ExitStack() as ph:
                ws = make_ws(ph, 4)
                ws.plan(fm_pieces(w_ff1[layer], [c * 128 for c in range(8)]))
                ws._pump()
                ln_tiles(layer * 2, tiles, True, ph, nxb=1)
                if dbg:
                    dump_res(dbg + "_ln0")
                rt = [(sb("ffn_rt%d" % i, [128, 512], F32, ph), Buf()) for i in range(2)]
                hid = QX[:, :, :].rearrange("p (a k) t -> p a k t", a=2)
                B_h = [Buf(), Buf()]
                groups = []
                t0 = 0
                while t0 < ntok:
                    n = min(512, ntok - t0)
                    groups.append((t0, n, None))
                    t0 += n
                c2 = {"n": 0}
                for hb in range(8):
                    hbuf = hid[:, hb % 2]
                    bh = B_h[hb % 2]

                    def cons1(ci, gi, pb, hbuf=hbuf, bh=bh):
                        c0, n, _ = groups[gi]
                        k = c2["n"] % 2
                        c2["n"] += 1
                        T.op("act", lambda a: a.activation(out=rt[k][0][:, 0:n], in_=pb.ap[:, 0:n], func=AF.Relu),
                             [pb.buf], [rt[k][1]])
                        T.op("dve", lambda v: v.tensor_tensor(out=hbuf[:, ci, c0:c0 + n], in0=rt[k][0][:, 0:n],
                                                              in1=rt[k][0][:, 0:n], op=ALU.mult), [rt[k][1]], [bh])
                    linear_fm(ws, xT, lambda gi: [B_xT[t] for t in tiles[groups[gi][0] // 128:(groups[gi][0] + groups[gi][1]) // 128]],
                              w_ff1[layer],
                              [hb * 1024 + c * 128 for c in range(8)], groups, cons1, preplanned=(hb == 0))

                    def cons2(bi, si, pb):
                        ti = tiles[si]
                        rv = resid[:, ti, bi * 256:(bi + 1) * 256]
                        T.op("dve", lambda v: v.tensor_tensor(out=rv, in0=rv, in1=pb.ap[:, 0:256], op=ALU.add),
                             [pb.buf, B_res[ti]], [B_res[ti]])
                    segs2 = [(ti * 128, 128, [bh]) for ti in tiles]
                    linear_tm(ws, hbuf, w_ff2[layer][hb * 1024:(hb + 1) * 1024, :], 8, [i * 256 for i in range(8)],
                              segs2, cons2)
                T.barrier()
            if do_ln2:
                layer_norm_phase(layer * 2 + 1, tiles, False)

        def load_tokens(src_tiles, tiles, halo_src):
            with ExitStack() as ph:
                xb = [(sb("ld_xbf%d" % i, [128, D], BF16, ph), Buf()) for i in range(2)]
                hst = sb("ld_hst", [16, D], F32, ph)
                bh = Buf()
                for ti, src in zip(tiles, src_tiles):
                    T.dma("sp", out=resid[:, ti, :], in_=src, writes=[B_res[ti]])
                T.dma("sp", out=hst[:], in_=halo_src, writes=[bh])
                for ti in tiles:
                    make_xT_tile(xb, resid[:, ti, :], [B_res[ti]], xT, ti * 128, [B_xT[ti]])
                make_xT_tile(xb, hst[:], [bh], xTh, 0, [B_xTh], nrows=16)
                T.barrier()

        def layer0_front(tiles, with_samples, save_state):
            ntok = len(tiles) * 128
            with ExitStack() as ph:
                ws = make_ws(ph)
                xin_t = sb("xin_t", [128, NTOK], F32, ph)
                cu = sb("cu", [128, NTOK], F32, ph)
                vbp = sb("vbp", [128, 8, 130], F32, ph)
                vbs = sb("vbs", [128, 4, 66], F32, ph)
                xh = sb("xh", [128, 16], F32, ph)
                scst = sb("scst", [128, 12, 8], F32, ph)
                vlast = sb("vlast", [128, 12, 10], F32, ph)
                b_xin, b_cu, b_v, b_xh, b_sc, b_vl = Buf(), Buf(), Buf(), Buf(), Buf(), Buf()
                if with_samples:
                    for j in range(12):
                        T.dma("sp", out=scst[:, j, :], in_=sconv[:, j * 128:(j + 1) * 128].rearrange("r p -> p r"),
                              writes=[b_sc], allow_slow_non_contiguous=True)
                groups = [(0, 512, None), (512, 512, None)]
                if with_samples:
                    groups.append((1024, 256, None))
                groups.append((0, 16, xTh))
                cols = []
                kinds = []
                for j in range(12):
                    cols += [j * 128, 3072 + j * 128, 1536 + j * 128]
                    kinds += [("xin", j), ("gc", j), ("gb", j)]
                for m in range(4):
                    cols.append(4608 + m * 128)
                    kinds.append(("qm", m))
                ngr = len(groups)

                def src_bufs(gi):
                    if gi == ngr - 1:
                        return [B_xTh]
                    c0, n, _ = groups[gi]
                    return [B_xT[t] for t in range(c0 // 128, (c0 + n) // 128)]

                def qx_bufs(gi):
                    c0, n, _ = groups[gi]
                    return [B_qx[t] for t in range(c0 // 128, (c0 + n) // 128)]

                def cons(ci, gi, pb):
                    kind, j = kinds[ci]
                    c0, n, _ = groups[gi]
                    halo = (gi == ngr - 1)
                    if kind == "xin":
                        if halo:
                            T.op("act", lambda a: a.activation(out=xh[:], in_=pb.ap[:, 0:16], func=AF.Copy), [pb.buf], [b_xh])
                        else:
                            T.op("act", lambda a: a.activation(out=xin_t[:, c0:c0 + n], in_=pb.ap[:, 0:n], func=AF.Copy),
                                 [pb.buf], [b_xin])
                    elif kind == "gc":
                        if halo:
                            T.op("dve", lambda v: v.tensor_tensor(out=vbp[:, :, 0:2],
                                                                  in0=pb.ap[:, 0:16].rearrange("p (a b) -> p a b", b=2),
                                                                  in1=xh[:].rearrange("p (a b) -> p a b", b=2), op=ALU.mult),
                                 [pb.buf, b_xh], [b_v])
                            if with_samples:
                                T.op("act", lambda a: a.activation(out=vbs[:, :, 0:2],
                                                                   in_=scst[:, j, :].rearrange("p (a b) -> p a b", b=2),
                                                                   func=AF.Copy), [b_sc], [b_v])
                            sets = [(cu[:, 0:1024].rearrange("p (a b) -> p a b", b=128), vbp, 128)]
                            if with_samples:
                                sets.append((cu[:, 1024:1280].rearrange("p (a b) -> p a b", b=64), vbs, 64))
                            for (cv, vb, L) in sets:
                                T.op("dve", lambda v: v.tensor_scalar(out=cv, in0=vb[:, :, 2:2 + L], scalar1=cw[:, j, 2:3],
                                                                      scalar2=None, op0=ALU.mult), [b_v, B_const], [b_cu])
                                T.op("dve", lambda v: v.scalar_tensor_tensor(out=cv, in0=vb[:, :, 1:1 + L], scalar=cw[:, j, 1:2],
                                                                             in1=cv, op0=ALU.mult, op1=ALU.add), [b_v, b_cu], [b_cu])
                                T.op("dve", lambda v: v.scalar_tensor_tensor(out=cv, in0=vb[:, :, 0:L], scalar=cw[:, j, 0:1],
                                                                             in1=cv, op0=ALU.mult, op1=ALU.add), [b_v, b_cu], [b_cu])
                            if save_state:
                                T.op("act", lambda a: a.activation(out=vlast[:, j, 0:2], in_=vbp[:, 7, 128:130], func=AF.Copy),
                                     [b_v], [b_vl])
                                T.op("act", lambda a: a.activation(out=vlast[:, j, 2:10].rearrange("p (a b) -> p a b", b=2),
                                                                   in_=vbs[:, :, 64:66], func=AF.Copy), [b_v], [b_vl])
                        elif n == 512:
                            g4 = c0 // 128
                            T.op("dve", lambda v: v.tensor_tensor(out=vbp[:, g4:g4 + 4, 2:130],
                                                                  in0=pb.ap[:, 0:512].rearrange("p (a b) -> p a b", b=128),
                                                                  in1=xin_t[:, c0:c0 + 512].rearrange("p (a b) -> p a b", b=128),
                                                                  op=ALU.mult), [pb.buf, b_xin, b_cu], [b_v])
                        else:
                            T.op("dve", lambda v: v.tensor_tensor(out=vbs[:, :, 2:66],
                                                                  in0=pb.ap[:, 0:256].rearrange("p (a b) -> p a b", b=64),
                                                                  in1=xin_t[:, c0:c0 + 256].rearrange("p (a b) -> p a b", b=64),
                                                                  op=ALU.mult), [pb.buf, b_xin, b_cu], [b_v])
                    elif kind == "gb":
                        if not halo:
                            T.op("dve", lambda v: v.tensor_tensor(out=QX[:, j, c0:c0 + n], in0=pb.ap[:, 0:n],
                                                                  in1=cu[:, c0:c0 + n], op=ALU.mult),
                                 [pb.buf, b_cu], qx_bufs(gi))
                    else:
                        if not halo:
                            copy_op("act", QX[:, 12 + j, c0:c0 + n], pb.ap[:, 0:n], [pb.buf], qx_bufs(gi))
                linear_fm(ws, xT, src_bufs, w_in_a, cols, groups, cons)
                if save_state:
                    for j in range(12):
                        T.dma("sp", out=conv_out[:, j * 128:(j + 1) * 128].rearrange("r p -> p r"), in_=vlast[:, j, :],
                              reads=[b_vl], allow_slow_non_contiguous=True)
                T.barrier()

        def kv_project(tiles, dsts, ln_idx, do_inb):
            with ExitStack() as ph:
                ws = make_ws(ph, 4)
                stg = [(sb("kv_stg%d" % i, [128, 256], F32, ph), Buf()) for i in range(4)]
                c = {"n": 0}
                segs = [(ti * 128, 128, [B_xT[ti]]) for ti in tiles]
                ws.plan(tm_pieces(w_kv, 16, [i * 256 for i in range(12)]))
                ws._pump()
                ln_tiles(ln_idx, tiles, False, ph, nxb=1)

                def cons(bi, si, pb):
                    k = c["n"] % 4
                    c["n"] += 1
                    copy_op(evac_eng(), stg[k][0][:], pb.ap[:, 0:256], [pb.buf], [stg[k][1]])
                    T.dma("sp", out=dsts[si][:, bi * 256:(bi + 1) * 256], in_=stg[k][0][:], reads=[stg[k][1]])
                linear_tm(ws, xT, w_kv, 16, [i * 256 for i in range(12)], segs, cons, preplanned=True)
                if do_inb:
                    groups = [(0, 512, None), (512, 512, None), (1024, 256, None)]

                    def cons_b(ci, gi, pb):
                        c0, n, _ = groups[gi]
                        copy_op(evac_eng(), QX[:, ci, c0:c0 + n], pb.ap[:, 0:n], [pb.buf],
                                [B_qx[t] for t in range(c0 // 128, (c0 + n) // 128)])
                    linear_fm(ws, xT, lambda gi: [B_xT[t] for t in range(groups[gi][0] // 128, (groups[gi][0] + groups[gi][1]) // 128)],
                              w_in_b, [cc * 128 for cc in range(16)], groups, cons_b)
                T.barrier()

        B_kvd = Buf()
        B_memd = Buf()

        def dump_qx(name):
            if not DEBUG:
                return
            d = nc.dram_tensor("dbg_" + name, [128, 16, NTOK], F32, kind="ExternalOutput").ap()
            T.dma("pool", out=d, in_=QX[:], reads=B_qx)
            T.barrier()

        def dump_res(name):
            if not DEBUG:
                return
            d = nc.dram_tensor("dbg_" + name, [128, NT, D], F32, kind="ExternalOutput").ap()
            T.dma("sp", out=d, in_=resid[:], reads=B_res)
            T.barrier()

        def mem_kv_phase():
            with ExitStack() as ph:
                ws = make_ws(ph, 4)
                mst = sb("mst", [128, D], F32, ph)
                memT = sb("memT", [128, 16, 256], BF16, ph)
                xb1 = (sb("mk_xbf", [128, D], BF16, ph), Buf())
                xb = [xb1, xb1]
                stg = [(sb("mk_stg%d" % i, [128, 256], F32, ph), Buf()) for i in range(4)]
                bm, bmt = Buf(), Buf()
                for mc in range(2):
                    T.dma("sp", out=mst[:], in_=mem_p[mc * 128:(mc + 1) * 128, :], writes=[bm])
                    make_xT_tile(xb, mst[:], [bm], memT, mc * 128, [bmt])
                c = {"n": 0}
                for l in range(2):
                    def cons(bi, si, pb, l=l):
                        k = c["n"] % 4
                        c["n"] += 1
                        copy_op(evac_eng(), stg[k][0][:], pb.ap[:, 0:256], [pb.buf], [stg[k][1]])
                        T.dma("sp", out=memkv_p[l, si * 128:(si + 1) * 128, bi * 256:(bi + 1) * 256], in_=stg[k][0][:],
                              reads=[stg[k][1]])
                    linear_tm(ws, memT, w_mem_kv[l], 16, [i * 256 for i in range(4)],
                              [(0, 128, [bmt]), (128, 128, [bmt])], cons)
                T.barrier()

        def mem_segs(layer, tiles):
            segs = []
            for ti in tiles:
                if ti < 8:
                    segs.append((ti * 128, 128, ti, "p", memkv_p[layer, :, 0:512], memkv_p[layer, :, 512:1024]))
            for ti in tiles:
                if ti >= 8:
                    for hf in range(2):
                        s = (ti - 8) * 2 + hf
                        segs.append((ti * 128 + hf * 64, 64, ti, "s%d" % s, cmk[layer, s], cmv[layer, s]))
            return segs

        def stick_breaking():
            with ExitStack() as ph:
                Kb = [(sb("Kb%d" % i, [128, DMAIN], BF16, ph), Buf()) for i in range(2)]
                Vb = [(sb("Vb%d" % i, [128, DMAIN], BF16, ph), Buf()) for i in range(3)]
                KT = [(sb("KT%d" % i, [128, 12, 128], BF16, ph), Buf()) for i in range(2)]
                Eb = [(sb("Eb%d" % i, [128, 512], F32, ph), Buf()) for i in range(2)]
                Sb = [(sb("Sb%d" % i, [128, 512], F32, ph), Buf()) for i in range(2)]
                Xb1 = (sb("Xb", [128, 512], F32, ph), Buf())
                Xb = [Xb1, Xb1]
                Srem = [(sb("Srem%d" % i, [128, 512], BF16, ph), Buf()) for i in range(2)]
                SPrem = [(sb("SPrem%d" % i, [128, 512], BF16, ph), Buf()) for i in range(3)]

                def hi_view(ap32):
                    return ap32.bitcast(BF16).rearrange("p (n two) -> p n two", two=2)[:, :, 1]

                Ab = [(sb("Ab%d" % i, [128, 512], BF16, ph), Buf()) for i in range(2)]
                SP = [(sb("SPacc%d" % i, [128, 512], F32, ph), Buf()) for i in range(3)]
                oacc = pbanks[0:3]
                rot = pbanks[3:6]
                rix = {"n": 0, "u": 0, "kb": 0}

                def rpsum():
                    b = rot[rix["n"] % 3]
                    rix["n"] += 1
                    return b

                def attend(c0, nq, ti, blocks):
                    N = 4 * nq
                    nb = len(blocks)
                    nu = 3 * nb
                    kbase = rix["kb"]
                    rix["kb"] += nb
                    for hg in range(3):
                        T.op("dve", lambda v: v.memset(SP[hg][0][:, 0:N], 0.0), [], [SP[hg][1]])
                    st = {}

                    def load(bi):
                        k_d, v_d, nk, mask = blocks[bi]
                        kk = (kbase + bi) % 3
                        k2 = (kbase + bi) % 2
                        T.dma("pool", out=Kb[k2][0][0:nk, :], in_=k_d, reads=[B_kvd], writes=[Kb[k2][1]])
                        T.dma("pool", out=Vb[kk][0][0:nk, :], in_=v_d, reads=[B_kvd], writes=[Vb[kk][1]])

                    def transp(bi):
                        k_d, v_d, nk, mask = blocks[bi]
                        kk = (kbase + bi) % 2
                        kt = (kbase + bi) % 2
                        for half, (h0, hn) in enumerate(((0, 8), (8, 4))):
                            tb = tpsum()
                            T.group("pe", [
                                (lambda pe, hh=hh: pe.transpose(out=tb.ap[:, (hh - h0) * 128:(hh - h0) * 128 + nk],
                                                                in_=Kb[kk][0][0:nk, hh * 128:(hh + 1) * 128],
                                                                identity=identb[0:nk, 0:nk]))
                                for hh in range(h0, h0 + hn)], [Kb[kk][1], B_const], [tb.buf])
                            copy_op(evac_eng(), KT[kt][0][:, h0:h0 + hn, 0:nk],
                                    tb.ap[:, 0:hn * 128].rearrange("p (a b) -> p a b", b=128)[:, :, 0:nk],
                                    [tb.buf], [KT[kt][1]])

                    def stageA(u):
                        bi, hg = divmod(u, 3)
                        k_d, v_d, nk, mask = blocks[bi]
                        kt = (kbase + bi) % 2
                        E, S = Eb[u % 2], Sb[u % 2]
                        zb = rpsum()
                        T.group("pe", [
                            (lambda pe, hh=hh: pe.matmul(zb.ap[0:nk, hh * nq:(hh + 1) * nq],
                                                         lhsT=KT[kt][0][:, hg * 4 + hh, 0:nk],
                                                         rhs=QX[:, hg * 4 + hh, c0:c0 + nq], start=True, stop=True))
                            for hh in range(4)], [KT[kt][1], B_qx[ti]], [zb.buf])
                        T.op("act", lambda a: a.activation(out=E[0][0:nk, 0:N], in_=zb.ap[0:nk, 0:N], func=AF.Exp,
                                                           scale=SCALE), [zb.buf], [E[1]])
                        T.op("act", lambda a: a.activation(out=S[0][0:nk, 0:N], in_=E[0][0:nk, 0:N], func=AF.Ln,
                                                           bias=1.0), [E[1]], [S[1]])
                        if mask is not None:
                            for hh in range(4):
                                T.op("dve", lambda v, hh=hh: v.tensor_tensor(out=S[0][0:nk, hh * nq:(hh + 1) * nq],
                                                                             in0=S[0][0:nk, hh * nq:(hh + 1) * nq],
                                                                             in1=mask[0:nk, 0:nq], op=ALU.mult),
                                     [S[1], B_const], [S[1]])
                        R = Srem[u % 2]
                        T.op("dve", lambda v: v.tensor_tensor(out=R[0][0:nk, 0:N], in0=S[0][0:nk, 0:N],
                                                              in1=hi_view(S[0][0:nk, 0:N]), op=ALU.subtract), [S[1]], [R[1]])

                    def stageB(u):
                        bi, hg = divmod(u, 3)
                        k_d, v_d, nk, mask = blocks[bi]
                        E, S, X, A = Eb[u % 2], Sb[u % 2], Xb[u % 2], Ab[u % 2]
                        cb = rpsum()
                        R = Srem[u % 2]
                        fns = [lambda pe: pe.matmul(cb.ap[0:nk, 0:N], lhsT=tri[0:nk, 0:nk], rhs=hi_view(S[0][0:nk, 0:N]),
                                                    start=True, stop=False),
                               lambda pe: pe.matmul(cb.ap[0:nk, 0:N], lhsT=tri[0:nk, 0:nk], rhs=R[0][0:nk, 0:N],
                                                    start=False, stop=(bi == 0))]
                        rds = [S[1], R[1], B_const]
                        if bi > 0:
                            fns.append(lambda pe: pe.matmul(cb.ap[0:nk, 0:N], lhsT=ones[:, 0:nk],
                                                            rhs=hi_view(SP[hg][0][:, 0:N]), start=False, stop=False))
                            fns.append(lambda pe: pe.matmul(cb.ap[0:nk, 0:N], lhsT=ones[:, 0:nk],
                                                            rhs=SPrem[hg][0][:, 0:N], start=False, stop=True))
                            rds += [SP[hg][1], SPrem[hg][1]]
                        T.group("pe", fns, rds, [cb.buf])
                        T.op("act", lambda a: a.activation(out=X[0][0:nk, 0:N], in_=cb.ap[0:nk, 0:N], func=AF.Exp,
                                                           scale=-1.0), [cb.buf], [X[1]])
                        T.op("dve", lambda v: v.tensor_tensor(out=A[0][0:nk, 0:N], in0=E[0][0:nk, 0:N],
                                                              in1=X[0][0:nk, 0:N], op=ALU.mult), [E[1], X[1]], [A[1]])
                        if mask is not None:
                            for hh in range(4):
                                T.op("dve", lambda v, hh=hh: v.tensor_tensor(out=A[0][0:nk, hh * nq:(hh + 1) * nq],
                                                                             in0=A[0][0:nk, hh * nq:(hh + 1) * nq],
                                                                             in1=mask[0:nk, 0:nq], op=ALU.mult),
                                     [A[1], B_const], [A[1]])
                        if bi < nb - 1:
                            T.op("dve", lambda v: v.tensor_tensor(out=SP[hg][0][0:nk, 0:N], in0=SP[hg][0][0:nk, 0:N],
                                                                  in1=S[0][0:nk, 0:N], op=ALU.add),
                                 [S[1], SP[hg][1]], [SP[hg][1]])
                            T.op("pool", lambda g: g.tensor_tensor(out=SPrem[hg][0][:, 0:N], in0=SP[hg][0][:, 0:N],
                                                                   in1=hi_view(SP[hg][0][:, 0:N]), op=ALU.subtract),
                                 [SP[hg][1]], [SPrem[hg][1]])

                    def stageC(u):
                        bi, hg = divmod(u, 3)
                        k_d, v_d, nk, mask = blocks[bi]
                        kk = (kbase + bi) % 3
                        A = Ab[u % 2]
                        T.group("pe", [
                            (lambda pe, hh=hh: pe.matmul(oacc[hg].ap[:, hh * nq:(hh + 1) * nq],
                                                         lhsT=Vb[kk][0][0:nk, (hg * 4 + hh) * 128:(hg * 4 + hh + 1) * 128],
                                                         rhs=A[0][0:nk, hh * nq:(hh + 1) * nq],
                                                         start=(bi == 0 and hh == 0), stop=(bi == nb - 1),
                                                         skip_group_check=True))
                            for hh in range(4)], [Vb[kk][1], A[1]], [oacc[hg].buf])

                    load(0)
                    transp(0)
                    if nb > 1:
                        load(1)
                    for t in range(nu + 2):
                        if t < nu:
                            stageA(t)
                            bi, hg = divmod(t, 3)
                            if hg == 1 and bi + 1 < nb:
                                transp(bi + 1)
                        if 0 <= t - 1 < nu:
                            stageB(t - 1)
                        if 0 <= t - 2 < nu:
                            stageC(t - 2)
                        if t < nu:
                            bi, hg = divmod(t, 3)
                            if hg == 1 and bi + 2 < nb:
                                load(bi + 2)
                    for hg in range(3):
                        copy_op(evac_eng(), QX[:, hg * 4:hg * 4 + 4, c0:c0 + nq],
                                oacc[hg].ap[:, 0:N].rearrange("p (a b) -> p a b", b=nq), [oacc[hg].buf], [B_qx[ti]])

                for s in range(4):
                    ti = 8 + s // 2
                    r0 = (s % 2) * 64
                    blocks = [(kv_smp[s // 2, r0:r0 + 64, 0:DMAIN], kv_smp[s // 2, r0:r0 + 64, DMAIN:3072], 64, mtri)]
                    for kb in range(31, -1, -1):
                        blocks.append((cache_k[s, kb * 128:(kb + 1) * 128, :], cache_v[s, kb * 128:(kb + 1) * 128, :], 128, None))
                    attend(ti * 128 + r0, 64, ti, blocks)
                for j in range(8):
                    blocks = []
                    for jj in range(j, -1, -1):
                        blocks.append((kv_own[jj, :, 0:DMAIN], kv_own[jj, :, DMAIN:3072], 128, mtri if jj == j else None))
                        blocks.append((kv_oth[jj, :, 0:DMAIN], kv_oth[jj, :, DMAIN:3072], 128, m0 if jj == 0 else None))
                    attend(j * 128, 128, j, blocks)
                T.barrier()

        def in_b_project(tiles):
            ntok = len(tiles) * 128
            with ExitStack() as ph:
                ws = make_ws(ph)
                groups = [(0, 512, None), (512, 512, None), (1024, 256, None)]

                def cons(ci, gi, pb):
                    c0, n, _ = groups[gi]
                    copy_op(evac_eng(), QX[:, ci, c0:c0 + n], pb.ap[:, 0:n], [pb.buf],
                            [B_qx[t] for t in range(c0 // 128, (c0 + n) // 128)])
                linear_fm(ws, xT, lambda gi: [B_xT[t] for t in range(groups[gi][0] // 128, (groups[gi][0] + groups[gi][1]) // 128)],
                          w_in_b, [c * 128 for c in range(16)], groups, cons)
                T.barrier()

        mem_kv_phase()
        P8 = list(range(8))
        P10 = list(range(10))
        load_tokens([x_oth[i] for i in range(8)], P8, h_oth)
        layer0_front(P8, False, False)
        mem_attention(0, mem_segs(0, P8))
        out_ln_ffn(0, P8, do_ln2=False)
        kv_project(P8, [kv_oth[i] for i in range(8)], 1, False)
        load_tokens([x_own[i] for i in range(8)] + [x_smp[i] for i in range(2)], P10, h_own)
        layer0_front(P10, True, True)
        mem_attention(0, mem_segs(0, P10))
        dump_qx("mixin0")
        out_ln_ffn(0, P10, "l0", do_ln2=False)
        kv_project(P10, [kv_own[i] for i in range(8)] + [kv_smp[i] for i in range(2)], 1, True)
        dump_qx("q")
        stick_breaking()
        mem_attention(1, mem_segs(1, P10))
        dump_qx("mixin1")
        out_ln_ffn(1, P10)
        for ti in P10:
            T.dma("sp", out=y_out[ti], in_=resid[:, ti, :], reads=[B_res[ti]])
        T.barrier()
    return nc


_CACHE = {}


def kernel(x_prompt, x_sample, state_conv, cache_k, cache_v, cache_mem_k, cache_mem_v, mem_prompt,
           w_in_a, conv_w, w_in_b, w_kv, w_mem_kv, w_out, w_ff1, w_ff2, ln_g, ln_b):
    f = lambda a: np.ascontiguousarray(np.asarray(a, dtype=np.float32))
    x_prompt, x_sample, state_conv = f(x_prompt), f(x_sample), f(state_conv)
    cache_k, cache_v, cache_mem_k, cache_mem_v, mem_prompt = f(cache_k), f(cache_v), f(cache_mem_k), f(cache_mem_v), f(mem_prompt)
    if "nc" not in _CACHE:
        _CACHE["nc"] = build_program()
    nc = _CACHE["nc"]
    ident = np.eye(128, dtype=np.float32)
    jj, ss = np.meshgrid(np.arange(128), np.arange(128), indexing="ij")
    tri = (jj >= ss).astype(np.float32)
    mtri = (jj < ss).astype(np.float32)
    ones = np.ones((128, 128), np.float32)
    shared = {
        "w_in_a": f(w_in_a)[0], "conv_w": f(conv_w)[0], "w_in_b": f(w_in_b)[0], "w_kv": f(w_kv),
        "w_mem_kv": f(w_mem_kv), "w_out": f(w_out), "w_ff1": f(w_ff1), "w_ff2": f(w_ff2),
        "ln_g": f(ln_g).reshape(4, D), "ln_b": f(ln_b).reshape(4, D),
        "c_ident": ident, "c_tri": tri, "c_ones": ones, "c_mtri": mtri,
    }
    xp = x_prompt.reshape(4, 16, 128, D)
    zeros2 = np.zeros((2, D), np.float32)

    def halo(b, g):
        return zeros2 if g == 0 else x_prompt[b, g * 128 - 2:g * 128]

    in_maps = []
    own_blocks, oth_blocks = [], []
    for c in range(NCORES):
        b, h = c // 2, c % 2
        own = [2 * j + h for j in range(8)]
        oth = [2 * j for j in range(8)] if h == 1 else [15] + [2 * j - 1 for j in range(1, 8)]
        own_blocks.append(own)
        oth_blocks.append(oth)
        m = dict(shared)
        m["x_own"] = np.ascontiguousarray(xp[b, own])
        m["x_oth"] = np.ascontiguousarray(xp[b, oth])
        m["x_smp"] = np.ascontiguousarray(x_sample[4 * c:4 * c + 4].reshape(2, 128, D))
        m["h_own"] = np.ascontiguousarray(np.concatenate([halo(b, g) for g in own], axis=0))
        m["h_oth"] = np.ascontiguousarray(np.concatenate([halo(b, g) for g in oth], axis=0))
        m["sconv"] = np.ascontiguousarray(state_conv[0, 4 * c:4 * c + 4].reshape(8, DMAIN))
        m["cache_k"] = np.ascontiguousarray(cache_k[4 * c:4 * c + 4].reshape(4, 4096, DMAIN))
        m["cache_v"] = np.ascontiguousarray(cache_v[4 * c:4 * c + 4].reshape(4, 4096, DMAIN))
        m["cmk"] = np.ascontiguousarray(cache_mem_k[:, 4 * c:4 * c + 4].reshape(2, 4, NMEM, DMEM))
        m["cmv"] = np.ascontiguousarray(cache_mem_v[:, 4 * c:4 * c + 4].reshape(2, 4, NMEM, DMEM))
        m["mem_p"] = np.ascontiguousarray(mem_prompt[b])
        m["c_m0"] = ones if h == 1 else np.zeros((128, 128), np.float32)
        in_maps.append(m)
    res = run_bass_kernel_spmd(nc, in_maps, core_ids=list(range(NCORES)))
    R = res.results
    if DEBUG:
        _CACHE["dbg"] = {k: np.asarray(v) for k, v in R[0].items() if k.startswith("dbg_") or k in ("kv_own", "kv_smp", "kv_oth", "memkv_p", "y_out")}
    y_prompt = np.zeros((4, 16, 128, D), np.float32)
    y_sample = np.zeros((32, 64, D), np.float32)
    conv_p = np.zeros((1, 4, 2, DMAIN), np.float32)
    conv_s = np.zeros((1, 32, 2, DMAIN), np.float32)
    kvp = np.zeros((4, 16, 128, 3072), np.float32)
    kvs = np.zeros((32, 64, 3072), np.float32)
    mkp = np.zeros((2, 4, NMEM, DMEM), np.float32)
    mvp = np.zeros((2, 4, NMEM, DMEM), np.float32)
    for c in range(NCORES):
        b, h = c // 2, c % 2
        r = R[c]
        y = np.asarray(r["y_out"])
        for j, g in enumerate(own_blocks[c]):
            y_prompt[b, g] = y[j]
            kvp[b, g] = np.asarray(r["kv_own"])[j]
        y_sample[4 * c:4 * c + 4] = y[8:10].reshape(4, 64, D)
        kvs[4 * c:4 * c + 4] = np.asarray(r["kv_smp"]).reshape(4, 64, 3072)
        co = np.asarray(r["conv_out"])
        conv_s[0, 4 * c:4 * c + 4] = co[2:10].reshape(4, 2, DMAIN)
        if h == 1:
            conv_p[0, b] = co[0:2]
            mk = np.asarray(r["memkv_p"])
            mkp[:, b] = mk[:, :, 0:512]
            mvp[:, b] = mk[:, :, 512:1024]
    y_prompt = y_prompt.reshape(4, 2048, D)
    kvp = kvp.reshape(4, 2048, 3072)
    k_p = np.ascontiguousarray(kvp[:, :, 0:DMAIN]).reshape(4, 2048, 12, 128)
    v_p = np.ascontiguousarray(kvp[:, :, DMAIN:]).reshape(4, 2048, 12, 128)
    k_s = np.ascontiguousarray(kvs[:, :, 0:DMAIN]).reshape(32, 64, 12, 128)
    v_s = np.ascontiguousarray(kvs[:, :, DMAIN:]).reshape(32, 64, 12, 128)
    return (y_prompt, y_sample, conv_p, conv_s, k_p, v_p, k_s, v_s,
            mkp.reshape(2, 4, NMEM, 4, 128), mvp.reshape(2, 4, NMEM, 4, 128))
```

```python
import os
import numpy as np
from contextlib import ExitStack
import concourse.bass as bass
import concourse.mybir as mybir
from concourse.bass_utils import run_bass_kernel_spmd

F32 = mybir.dt.float32
BF16 = mybir.dt.bfloat16
AF = mybir.ActivationFunctionType
ALU = mybir.AluOpType
AX = mybir.AxisListType

D = 2048
DMAIN = 1536
DMEM = 512
NMEM = 256
DFF = 8192
ALPHA = 4.0 ** 0.25
EPS = 1e-5
SCALE = 128.0 ** -0.5
NT = 10
NTOK = NT * 128
NCORES = 8
DEBUG = bool(os.environ.get("KDBG"))


class Buf:
    __slots__ = ("w", "r")

    def __init__(self):
        self.w = None
        self.r = {}


class PBank:
    def __init__(self, ap, buf):
        self.ap = ap
        self.buf = buf


class Trk:
    def __init__(self, nc, es):
        self.nc = nc
        self.eng = {"pe": nc.tensor, "act": nc.scalar, "dve": nc.vector, "pool": nc.gpsimd, "sp": nc.sync}
        self.sem = {k: es.enter_context(nc.semaphore("s_" + k)) for k in self.eng}
        self.cnt = {k: 0 for k in self.eng}
        self.waited = {k: {} for k in self.eng}
        self.dsems = {}
        self.didx = {}
        self.dval = {}
        for q, n in (("sp", 20), ("pool", 24)):
            self.dsems[q] = [es.enter_context(nc.semaphore("d_%s%d" % (q, i))) for i in range(n)]
            self.didx[q] = 0
            for i in range(n):
                self.dval[(q, i)] = 0
        self.last = {}
        self.dma_pending = {}

    def _wait(self, e, tok):
        if tok is None:
            return
        s, v, key = tok
        if self.waited[e].get(key, 0) >= v:
            return
        self.eng[e].wait_ge(s, v)
        self.waited[e][key] = v

    def _deps(self, e, reads, writes):
        for b in reads:
            self._wait(e, b.w)
        for b in writes:
            self._wait(e, b.w)
            for t in b.r.values():
                self._wait(e, t)

    def _commit(self, tok, reads, writes):
        for b in reads:
            b.r[tok[2]] = tok
        for b in writes:
            b.w = tok
            b.r = {}

    def op(self, e, fn, reads=(), writes=()):
        self._deps(e, reads, writes)
        inst = fn(self.eng[e])
        self.cnt[e] += 1
        inst.then_inc(self.sem[e], 1)
        tok = (self.sem[e], self.cnt[e], e)
        self.last[e] = tok
        self._commit(tok, reads, writes)
        return tok

    def group(self, e, fns, reads=(), writes=()):
        self._deps(e, reads, writes)
        inst = None
        for fn in fns:
            inst = fn(self.eng[e])
        self.cnt[e] += 1
        inst.then_inc(self.sem[e], 1)
        tok = (self.sem[e], self.cnt[e], e)
        self.last[e] = tok
        self._commit(tok, reads, writes)
        return tok

    def dma(self, q, out, in_, reads=(), writes=(), **kw):
        self._deps(q, reads, writes)
        i = self.didx[q]
        self.didx[q] = (i + 1) % len(self.dsems[q])
        s = self.dsems[q][i]
        key = (q, i)
        prev = self.dval[key]
        if prev:
            self._wait(q, (s, prev, key))
        self.eng[q].dma_start(out=out, in_=in_, **kw).then_inc(s, 16)
        self.dval[key] = prev + 16
        tok = (s, prev + 16, key)
        self.dma_pending[key] = tok
        self._commit(tok, reads, writes)
        return tok

    def barrier(self):
        toks = list(self.last.values()) + list(self.dma_pending.values())
        for e in self.eng:
            for t in toks:
                self._wait(e, t)
        self.dma_pending = {}


class WStream:
    NSLOT = 6

    def __init__(self, T, slots):
        self.T = T
        self.slots = slots
        self.NSLOT = len(slots)
        self.queue = []
        self.issued = 0
        self.taken = 0
        self.done = 0

    def plan(self, pieces):
        self.queue += pieces

    def release(self, n):
        self.done += n

    def _pump(self):
        lim = min(len(self.queue), self.done + self.NSLOT)
        while self.issued < lim:
            p = self.queue[self.issued]
            st, sb = self.slots[self.issued % self.NSLOT]
            for view, src in p:
                self.T.dma("pool", out=view(st), in_=src, writes=[sb])
            self.issued += 1

    def take(self):
        self._pump()
        st, sb = self.slots[self.taken % self.NSLOT]
        self.taken += 1
        return st, sb


def build_program():
    nc = bass.Bass("TRN2", target_bir_lowering=False)

    def din(name, shape):
        return nc.dram_tensor(name, list(shape), F32, kind="ExternalInput").ap()

    def dout(name, shape):
        return nc.dram_tensor(name, list(shape), F32, kind="ExternalOutput").ap()

    x_own = din("x_own", [8, 128, D])
    x_oth = din("x_oth", [8, 128, D])
    x_smp = din("x_smp", [2, 128, D])
    h_own = din("h_own", [16, D])
    h_oth = din("h_oth", [16, D])
    sconv = din("sconv", [8, DMAIN])
    cache_k = din("cache_k", [4, 4096, DMAIN])
    cache_v = din("cache_v", [4, 4096, DMAIN])
    cmk = din("cmk", [2, 4, NMEM, DMEM])
    cmv = din("cmv", [2, 4, NMEM, DMEM])
    mem_p = din("mem_p", [NMEM, D])
    w_in_a = din("w_in_a", [D, 5120])
    conv_w = din("conv_w", [DMAIN, 3])
    w_in_b = din("w_in_b", [D, D])
    w_kv = din("w_kv", [D, 3072])
    w_mem_kv = din("w_mem_kv", [2, D, 1024])
    w_out = din("w_out", [2, D, D])
    w_ff1 = din("w_ff1", [2, D, DFF])
    w_ff2 = din("w_ff2", [2, DFF, D])
    ln_g = din("ln_g", [4, D])
    ln_b = din("ln_b", [4, D])
    c_ident = din("c_ident", [128, 128])
    c_tri = din("c_tri", [128, 128])
    c_ones = din("c_ones", [128, 128])
    c_mtri = din("c_mtri", [128, 128])
    c_m0 = din("c_m0", [128, 128])

    y_out = dout("y_out", [NT, 128, D])
    kv_own = dout("kv_own", [8, 128, 3072])
    kv_oth = dout("kv_oth", [8, 128, 3072])
    kv_smp = dout("kv_smp", [2, 128, 3072])
    conv_out = dout("conv_out", [10, DMAIN])
    memkv_p = dout("memkv_p", [2, NMEM, 1024])

    with ExitStack() as es:
        T = Trk(nc, es)

        uid = {"n": 0}

        def sb(name, shape, dt, stack=es):
            uid["n"] += 1
            return stack.enter_context(nc.sbuf_tensor("%s_%d" % (name, uid["n"]), list(shape), dt))

        resid = sb("resid", [128, NT, D], F32)
        xT = sb("xT", [128, 16, NTOK], BF16)
        QX = sb("QX", [128, 16, NTOK], BF16)
        B_res = [Buf() for _ in range(NT)]
        B_xT = [Buf() for _ in range(NT)]
        B_qx = [Buf() for _ in range(NT)]
        identb = sb("identb", [128, 128], BF16)
        tri = sb("tri", [128, 128], F32)
        ones = sb("ones", [128, 128], F32)
        mtri = sb("mtri", [128, 128], F32)
        m0 = sb("m0", [128, 128], F32)
        cw = sb("cw", [128, 12, 3], F32)
        xTh = sb("xTh", [128, 16, 16], BF16)
        small = sb("small", [128, 64], F32)
        B_const = Buf()
        B_xTh = Buf()
        B_small = Buf()

        pbanks = []
        for i in range(6):
            t = es.enter_context(nc.psum_tensor("pb%d" % i, [128, 512], F32))
            pbanks.append(PBank(t, Buf()))
        tbanks = []
        for i in range(2):
            t = es.enter_context(nc.psum_tensor("tb%d" % i, [128, 1024], BF16))
            tbanks.append(PBank(t, Buf()))
        rr = {"p": 0, "t": 0, "e": 0}

        def psum(avoid=()):
            while True:
                b = pbanks[rr["p"] % 6]
                rr["p"] += 1
                if b not in avoid:
                    return b

        def tpsum():
            b = tbanks[rr["t"] % 2]
            rr["t"] += 1
            return b

        def evac_eng():
            rr["e"] += 1
            return "act" if rr["e"] % 2 else "dve"

        def copy_op(e, out, in_, reads, writes):
            if e == "act":
                return T.op("act", lambda a: a.activation(out=out, in_=in_, func=AF.Copy), reads, writes)
            return T.op("dve", lambda v: v.tensor_copy(out=out, in_=in_), reads, writes)

        with ExitStack() as ph:
            stg = sb("cstg", [128, 128], F32, ph)
            bs = Buf()
            T.dma("sp", out=stg[:], in_=c_ident, writes=[bs])
            T.op("dve", lambda v: v.tensor_copy(out=identb[:], in_=stg[:]), [bs], [B_const])
            T.dma("sp", out=tri[:], in_=c_tri, writes=[B_const])
            T.dma("sp", out=ones[:], in_=c_ones, writes=[B_const])
            T.dma("sp", out=mtri[:], in_=c_mtri, writes=[B_const])
            T.dma("sp", out=m0[:], in_=c_m0, writes=[B_const])
            T.dma("sp", out=cw[:], in_=conv_w.rearrange("(j p) k -> p j k", p=128), writes=[B_const],
                  allow_slow_non_contiguous=True)
            T.barrier()

        def make_xT_tile(ph_xbf, src_ap, src_bufs, dstT, dst_col, dst_bufs, nrows=128, ncolT=128):
            xbf, bx = ph_xbf[rr["e"] % 2]
            rr["e"] += 1
            T.op("act", lambda a: a.activation(out=xbf[0:nrows, :], in_=src_ap, func=AF.Copy), src_bufs, [bx])
            for half in range(2):
                tb = tpsum()
                T.group("pe", [
                    (lambda pe, kc=kc: pe.transpose(out=tb.ap[:, (kc % 8) * 128:(kc % 8) * 128 + nrows],
                                                    in_=xbf[0:nrows, kc * 128:(kc + 1) * 128],
                                                    identity=identb[0:nrows, 0:nrows]))
                    for kc in range(half * 8, half * 8 + 8)], [bx, B_const], [tb.buf])
                src3 = tb.ap[:, :].rearrange("p (a b) -> p a b", b=128)[:, :, 0:nrows]
                copy_op(evac_eng(), dstT[:, half * 8:half * 8 + 8, dst_col:dst_col + nrows], src3, [tb.buf], dst_bufs)

        def fm_pieces(W, cols, kc_n=16):
            pieces = []
            for col in cols:
                pieces.append([(lambda st, kc_n=kc_n: st[:, 0:kc_n * 128].rearrange("p (k c) -> p k c", c=128),
                                W[:, col:col + 128].rearrange("(k p) c -> p k c", p=128))])
            return pieces

        def linear_fm(ws, src, src_bufs_of, W, cols, groups, consume, kc_n=16, preplanned=False):
            if not preplanned:
                ws.plan(fm_pieces(W, cols, kc_n))
            for ci, col in enumerate(cols):
                st, sbuf = ws.take()
                wv = st[:, 0:kc_n * 128].rearrange("p (k c) -> p k c", c=128)
                for gi, (c0, n, s_ap) in enumerate(groups):
                    pb = psum()
                    sap = src if s_ap is None else s_ap
                    T.group("pe", [
                        (lambda pe, kc=kc: pe.matmul(pb.ap[:, 0:n], lhsT=wv[:, kc, :], rhs=sap[:, kc, c0:c0 + n],
                                                     start=(kc == 0), stop=(kc == kc_n - 1)))
                        for kc in range(kc_n)], [sbuf] + src_bufs_of(gi), [pb.buf])
                    consume(ci, gi, pb)
                ws.release(1)

        def tm_pieces(W, kc_n, cblocks):
            npc = (kc_n + 7) // 8
            pieces = []
            for col in cblocks:
                for pi in range(npc):
                    k0 = pi * 8
                    kn = min(8, kc_n - k0)
                    pieces.append([(lambda st, kn=kn: st[:, 0:kn * 256].rearrange("p (k c) -> p k c", c=256),
                                    W[k0 * 128:(k0 + kn) * 128, col:col + 256].rearrange("(k p) c -> p k c", p=128))])
            return pieces

        def linear_tm(ws, src, W, kc_n, cblocks, segs, consume, preplanned=False):
            npc = (kc_n + 7) // 8
            if not preplanned:
                ws.plan(tm_pieces(W, kc_n, cblocks))
            for bi, col in enumerate(cblocks):
                sl = [ws.take() for _ in range(npc)]
                for si, (c0, nt, sbufs) in enumerate(segs):
                    pb = psum()
                    fns = []
                    for kc in range(kc_n):
                        st = sl[kc // 8][0]
                        wv = st[:, 0:2048].rearrange("p (k c) -> p k c", c=256)
                        fns.append(lambda pe, kc=kc, wv=wv: pe.matmul(pb.ap[0:nt, 0:256], lhsT=src[:, kc, c0:c0 + nt],
                                                                      rhs=wv[:, kc % 8, :], start=(kc == 0),
                                                                      stop=(kc == kc_n - 1)))
                    T.group("pe", fns, [s[1] for s in sl] + list(sbufs), [pb.buf])
                    consume(bi, si, pb)
                ws.release(npc)

        def make_ws(ph, nslot=6):
            slots = []
            for i in range(nslot):
                slots.append((sb("wslot%d" % i, [128, 2048], BF16, ph), Buf()))
            return WStream(T, slots)

        def ln_tiles(idx, tiles, scale_after, ph, nxb=2):
            gt = sb("ln_gt", [128, D], F32, ph)
            bt = sb("ln_bt", [128, D], F32, ph)
            xb1 = [(sb("ln_xbf%d" % i, [128, D], BF16, ph), Buf()) for i in range(nxb)]
            xb = [xb1[i % nxb] for i in range(2)]
            lnsm = sb("ln_sm", [128, NT, 8], F32, ph)
            bnst = [(sb("ln_bnst%d" % i, [128, 4, 6], F32, ph), Buf()) for i in range(2)]
            bg, bsm = Buf(), Buf()
            T.dma("sp", out=gt[:], in_=ln_g[idx].partition_broadcast(128), writes=[bg])
            T.dma("sp", out=bt[:], in_=ln_b[idx].partition_broadcast(128), writes=[bg])
            nt_ = len(tiles)
            t0_ = tiles[0]
            for k, ti in enumerate(tiles):
                bs, bb = bnst[k % 2]
                for q in range(4):
                    T.op("dve", lambda v, q=q: v.bn_stats(out=bs[:, q, :], in_=resid[:, ti, q * 512:(q + 1) * 512]),
                         [B_res[ti]], [bb])
                T.op("dve", lambda v: v.bn_aggr(out=lnsm[:, ti, 0:2], in_=bs[:].rearrange("p a b -> p (a b)")), [bb], [bsm])
            sl = lnsm[:, t0_:t0_ + nt_, :]
            T.op("dve", lambda v: v.tensor_scalar(out=sl[:, :, 2:3], in0=sl[:, :, 1:2], scalar1=EPS, scalar2=None,
                                                  op0=ALU.add), [bsm], [bsm])
            T.op("act", lambda a: a.activation(out=sl[:, :, 3:4], in_=sl[:, :, 2:3], func=AF.Sqrt), [bsm], [bsm])
            T.op("dve", lambda v: v.reciprocal(out=sl[:, :, 4:5], in_=sl[:, :, 3:4]), [bsm], [bsm])
            T.op("dve", lambda v: v.scalar_tensor_tensor(out=sl[:, :, 5:6], in0=sl[:, :, 0:1], scalar=-1.0, in1=sl[:, :, 4:5],
                                                         op0=ALU.mult, op1=ALU.mult), [bsm], [bsm])
            def stage1(ti):
                r = resid[:, ti, :]
                T.op("act", lambda a: a.activation(out=r, in_=r, func=AF.Identity, bias=lnsm[:, ti, 5:6],
                                                   scale=lnsm[:, ti, 4:5]), [bsm, B_res[ti]], [B_res[ti]])
                T.op("dve", lambda v: v.tensor_tensor(out=r, in0=r, in1=gt[:], op=ALU.mult), [bg, B_res[ti]], [B_res[ti]])
                T.op("pool", lambda g: g.tensor_tensor(out=r, in0=r, in1=bt[:], op=ALU.add), [bg, B_res[ti]], [B_res[ti]])

            def stage2(ti):
                r = resid[:, ti, :]
                make_xT_tile(xb, r, [B_res[ti]], xT, ti * 128, [B_xT[ti]])
                if scale_after:
                    T.op("act", lambda a: a.activation(out=r, in_=r, func=AF.Copy, scale=ALPHA), [B_res[ti]], [B_res[ti]])

            for k in range(nt_ + 2):
                if k < nt_:
                    stage1(tiles[k])
                if 0 <= k - 2 < nt_:
                    stage2(tiles[k - 2])

        def layer_norm_phase(idx, tiles, scale_after):
            with ExitStack() as ph:
                ln_tiles(idx, tiles, scale_after, ph)
                T.barrier()

        def mem_attention(layer, segs):
            with ExitStack() as ph:
                mkb = [(sb("mkb%d" % i, [128, 2, 512], BF16, ph), Buf()) for i in range(2)]
                mvb = [(sb("mvb%d" % i, [128, 2, 512], BF16, ph), Buf()) for i in range(2)]
                mkT = [(sb("mkT%d" % i, [128, 4, 256], BF16, ph), Buf()) for i in range(2)]
                Pf = [(sb("Pf%d" % i, [128, 256], F32, ph), Buf()) for i in range(2)]
                Pb = [(sb("Pb%d" % i, [128, 256], BF16, ph), Buf()) for i in range(2)]
                PT = [(sb("PT%d" % i, [128, 2, 128], BF16, ph), Buf()) for i in range(2)]
                sm = sb("ma_small", [128, 16], F32, ph)
                bsm = Buf()
                def load_set(si, mk_d, mv_d):
                    T.dma("pool", out=mkb[si][0][:], in_=mk_d.rearrange("(m p) c -> p m c", p=128), writes=[mkb[si][1]])
                    T.dma("pool", out=mvb[si][0][:], in_=mv_d.rearrange("(m p) c -> p m c", p=128), writes=[mvb[si][1]])
                    tb = tpsum()
                    T.group("pe", [
                        (lambda pe, m=m, mc=mc: pe.transpose(out=tb.ap[:, (m * 2 + mc) * 128:(m * 2 + mc + 1) * 128],
                                                             in_=mkb[si][0][:, mc, m * 128:(m + 1) * 128],
                                                             identity=identb[:]))
                        for m in range(4) for mc in range(2)], [mkb[si][1], B_const], [tb.buf])
                    copy_op("dve", mkT[si][0][:].rearrange("p a b -> p (a b)"), tb.ap[:, :], [tb.buf], [mkT[si][1]])

                cur = {"key": None, "i": -1}
                cnt = 0
                units = []
                setload = {}
                bsmk = [Buf(), Buf()]
                for (c0, nt, ti, setkey, mk_d, mv_d) in segs:
                    if setkey != cur["key"]:
                        cur["key"] = setkey
                        cur["i"] += 1
                        si = cur["i"] % 2

                        def do_load(si=si, mk_d=mk_d, mv_d=mv_d):
                            load_set(si, mk_d, mv_d)
                        setload[(c0, nt, ti, si, 0, cnt % 2)] = do_load
                    si = cur["i"] % 2
                    for m in range(4):
                        k = cnt % 2
                        cnt += 1
                        units.append((c0, nt, ti, si, m, k))

                def ma_stage1(c0, nt, ti, si, m, k):
                    pb = psum()
                    T.group("pe", [lambda pe: pe.matmul(pb.ap[0:nt, 0:256], lhsT=QX[:, 12 + m, c0:c0 + nt],
                                                        rhs=mkT[si][0][:, m, :], start=True, stop=True)],
                            [B_qx[ti], mkT[si][1]], [pb.buf])
                    smk = sm[:, k * 8:(k + 1) * 8]
                    T.op("dve", lambda v: v.reduce_max(out=smk[0:nt, 0:1], in_=pb.ap[0:nt, 0:256], axis=AX.X),
                         [pb.buf], [bsmk[k]])
                    T.op("dve", lambda v: v.tensor_scalar(out=smk[0:nt, 1:2], in0=smk[0:nt, 0:1], scalar1=-SCALE,
                                                          scalar2=None, op0=ALU.mult), [bsmk[k]], [bsmk[k]])
                    T.op("act", lambda a: a.activation(out=Pf[k][0][0:nt, :], in_=pb.ap[0:nt, 0:256], func=AF.Exp,
                                                       bias=smk[0:nt, 1:2], scale=SCALE, accum_out=smk[0:nt, 2:3]),
                         [pb.buf, bsmk[k]], [Pf[k][1], bsmk[k]])
                    T.op("dve", lambda v: v.reciprocal(out=smk[0:nt, 3:4], in_=smk[0:nt, 2:3]), [bsmk[k]], [bsmk[k]])
                    T.op("dve", lambda v: v.tensor_scalar(out=Pb[k][0][0:nt, :], in0=Pf[k][0][0:nt, :],
                                                          scalar1=smk[0:nt, 3:4], scalar2=None, op0=ALU.mult),
                         [bsmk[k], Pf[k][1]], [Pb[k][1]])

                def ma_stage2(c0, nt, ti, si, m, k):
                    tb = tpsum()
                    T.group("pe", [
                        (lambda pe, mc=mc: pe.transpose(out=tb.ap[:, mc * 128:mc * 128 + nt],
                                                        in_=Pb[k][0][0:nt, mc * 128:(mc + 1) * 128],
                                                        identity=identb[0:nt, 0:nt]))
                        for mc in range(2)], [Pb[k][1], B_const], [tb.buf])
                    copy_op("act", PT[k][0][:, :, 0:nt], tb.ap[:, 0:256].rearrange("p (a b) -> p a b", b=128)[:, :, 0:nt],
                            [tb.buf], [PT[k][1]])
                    pb2 = psum()
                    T.group("pe", [
                        (lambda pe, mc=mc: pe.matmul(pb2.ap[:, 0:nt], lhsT=mvb[si][0][:, mc, m * 128:(m + 1) * 128],
                                                     rhs=PT[k][0][:, mc, 0:nt], start=(mc == 0), stop=(mc == 1)))
                        for mc in range(2)], [mvb[si][1], PT[k][1]], [pb2.buf])
                    copy_op("dve", QX[:, 12 + m, c0:c0 + nt], pb2.ap[:, 0:nt], [pb2.buf], [B_qx[ti]])

                for u in range(len(units) + 1):
                    if u < len(units):
                        if units[u] in setload:
                            setload[units[u]]()
                        ma_stage1(*units[u])
                    if u >= 1:
                        ma_stage2(*units[u - 1])
                T.barrier()

        def out_ln_ffn(layer, tiles, dbg=None, do_ln2=True):
            ntok = len(tiles) * 128
            segs = [(ti * 128, 128, [B_qx[ti]]) for ti in tiles]
            with ExitStack() as ph:
                ws = make_ws(ph)

                def cons(bi, si, pb):
                    ti = tiles[si]
                    rv = resid[:, ti, bi * 256:(bi + 1) * 256]
                    T.op("dve", lambda v: v.scalar_tensor_tensor(out=rv, in0=rv, scalar=ALPHA, in1=pb.ap[:, 0:256],
                                                                 op0=ALU.mult, op1=ALU.add), [pb.buf, B_res[ti]], [B_res[ti]])
                linear_tm(ws, QX, w_out[layer], 16, [i * 256 for i in range(8)], segs, cons)
                T.barrier()
            with ExitStack() as ph:
                ws = make_ws(ph, 4)
                ws.plan(fm_pieces(w_ff1[layer], [c * 128 for c in range(8)]))
                ws._pump()
                ln_tiles(layer * 2, tiles, True, ph, nxb=1)
                if dbg:
                    dump_res(dbg + "_ln0")
                rt = [(sb("ffn_rt%d" % i, [128, 512], F32, ph), Buf()) for i in range(2)]
                hid = QX[:, :, :].rearrange("p (a k) t -> p a k t", a=2)
                B_h = [Buf(), Buf()]
                groups = []
                t0 = 0
                while t0 < ntok:
                    n = min(512, ntok - t0)
                    groups.append((t0, n, None))
                    t0 += n
                c2 = {"n": 0}
                for hb in range(8):
                    hbuf = hid[:, hb % 2]
                    bh = B_h[hb % 2]

                    def cons1(ci, gi, pb, hbuf=hbuf, bh=bh):
                        c0, n, _ = groups[gi]
                        k = c2["n"] % 2
                        c2["n"] += 1
                        T.op("act", lambda a: a.activation(out=rt[k][0][:, 0:n], in_=pb.ap[:, 0:n], func=AF.Relu),
                             [pb.buf], [rt[k][1]])
                        T.op("dve", lambda v: v.tensor_tensor(out=hbuf[:, ci, c0:c0 + n], in0=rt[k][0][:, 0:n],
                                                              in1=rt[k][0][:, 0:n], op=ALU.mult), [rt[k][1]], [bh])
                    linear_fm(ws, xT, lambda gi: [B_xT[t] for t in tiles[groups[gi][0] // 128:(groups[gi][0] + groups[gi][1]) // 128]],
                              w_ff1[layer],
                              [hb * 1024 + c * 128 for c in range(8)], groups, cons1, preplanned=(hb == 0))

                    def cons2(bi, si, pb):
                        ti = tiles[si]
                        rv = resid[:, ti, bi * 256:(bi + 1) * 256]
                        T.op("dve", lambda v: v.tensor_tensor(out=rv, in0=rv, in1=pb.ap[:, 0:256], op=ALU.add),
                             [pb.buf, B_res[ti]], [B_res[ti]])
                    segs2 = [(ti * 128, 128, [bh]) for ti in tiles]
                    linear_tm(ws, hbuf, w_ff2[layer][hb * 1024:(hb + 1) * 1024, :], 8, [i * 256 for i in range(8)],
                              segs2, cons2)
                T.barrier()
            if do_ln2:
                layer_norm_phase(layer * 2 + 1, tiles, False)

        def load_tokens(src_tiles, tiles, halo_src):
            with ExitStack() as ph:
                xb = [(sb("ld_xbf%d" % i, [128, D], BF16, ph), Buf()) for i in range(2)]
                hst = sb("ld_hst", [16, D], F32, ph)
                bh = Buf()
                for ti, src in zip(tiles, src_tiles):
                    T.dma("sp", out=resid[:, ti, :], in_=src, writes=[B_res[ti]])
                T.dma("sp", out=hst[:], in_=halo_src, writes=[bh])
                for ti in tiles:
                    make_xT_tile(xb, resid[:, ti, :], [B_res[ti]], xT, ti * 128, [B_xT[ti]])
                make_xT_tile(xb, hst[:], [bh], xTh, 0, [B_xTh], nrows=16)
                T.barrier()

        def layer0_front(tiles, with_samples, save_state):
            ntok = len(tiles) * 128
            with ExitStack() as ph:
                ws = make_ws(ph)
                xin_t = sb("xin_t", [128, NTOK], F32, ph)
                cu = sb("cu", [128, NTOK], F32, ph)
                vbp = sb("vbp", [128, 8, 130], F32, ph)
                vbs = sb("vbs", [128, 4, 66], F32, ph)
                xh = sb("xh", [128, 16], F32, ph)
                scst = sb("scst", [128, 12, 8], F32, ph)
                vlast = sb("vlast", [128, 12, 10], F32, ph)
                b_xin, b_cu, b_v, b_xh, b_sc, b_vl = Buf(), Buf(), Buf(), Buf(), Buf(), Buf()
                if with_samples:
                    for j in range(12):
                        T.dma("sp", out=scst[:, j, :], in_=sconv[:, j * 128:(j + 1) * 128].rearrange("r p -> p r"),
                              writes=[b_sc], allow_slow_non_contiguous=True)
                groups = [(0, 512, None), (512, 512, None)]
                if with_samples:
                    groups.append((1024, 256, None))
                groups.append((0, 16, xTh))
                cols = []
                kinds = []
                for j in range(12):
                    cols += [j * 128, 3072 + j * 128, 1536 + j * 128]
                    kinds += [("xin", j), ("gc", j), ("gb", j)]
                for m in range(4):
                    cols.append(4608 + m * 128)
                    kinds.append(("qm", m))
                ngr = len(groups)

                def src_bufs(gi):
                    if gi == ngr - 1:
                        return [B_xTh]
                    c0, n, _ = groups[gi]
                    return [B_xT[t] for t in range(c0 // 128, (c0 + n) // 128)]

                def qx_bufs(gi):
                    c0, n, _ = groups[gi]
                    return [B_qx[t] for t in range(c0 // 128, (c0 + n) // 128)]

                def cons(ci, gi, pb):
                    kind, j = kinds[ci]
                    c0, n, _ = groups[gi]
                    halo = (gi == ngr - 1)
                    if kind == "xin":
                        if halo:
                            T.op("act", lambda a: a.activation(out=xh[:], in_=pb.ap[:, 0:16], func=AF.Copy), [pb.buf], [b_xh])
                        else:
                            T.op("act", lambda a: a.activation(out=xin_t[:, c0:c0 + n], in_=pb.ap[:, 0:n], func=AF.Copy),
                                 [pb.buf], [b_xin])
                    elif kind == "gc":
                        if halo:
                            T.op("dve", lambda v: v.tensor_tensor(out=vbp[:, :, 0:2],
                                                                  in0=pb.ap[:, 0:16].rearrange("p (a b) -> p a b", b=2),
                                                                  in1=xh[:].rearrange("p (a b) -> p a b", b=2), op=ALU.mult),
                                 [pb.buf, b_xh], [b_v])
                            if with_samples:
                                T.op("act", lambda a: a.activation(out=vbs[:, :, 0:2],
                                                                   in_=scst[:, j, :].rearrange("p (a b) -> p a b", b=2),
                                                                   func=AF.Copy), [b_sc], [b_v])
                            sets = [(cu[:, 0:1024].rearrange("p (a b) -> p a b", b=128), vbp, 128)]
                            if with_samples:
                                sets.append((cu[:, 1024:1280].rearrange("p (a b) -> p a b", b=64), vbs, 64))
                            for (cv, vb, L) in sets:
                                T.op("dve", lambda v: v.tensor_scalar(out=cv, in0=vb[:, :, 2:2 + L], scalar1=cw[:, j, 2:3],
                                                                      scalar2=None, op0=ALU.mult), [b_v, B_const], [b_cu])
                                T.op("dve", lambda v: v.scalar_tensor_tensor(out=cv, in0=vb[:, :, 1:1 + L], scalar=cw[:, j, 1:2],
                                                                             in1=cv, op0=ALU.mult, op1=ALU.add), [b_v, b_cu], [b_cu])
                                T.op("dve", lambda v: v.scalar_tensor_tensor(out=cv, in0=vb[:, :, 0:L], scalar=cw[:, j, 0:1],
                                                                             in1=cv, op0=ALU.mult, op1=ALU.add), [b_v, b_cu], [b_cu])
                            if save_state:
                                T.op("act", lambda a: a.activation(out=vlast[:, j, 0:2], in_=vbp[:, 7, 128:130], func=AF.Copy),
                                     [b_v], [b_vl])
                                T.op("act", lambda a: a.activation(out=vlast[:, j, 2:10].rearrange("p (a b) -> p a b", b=2),
                                                                   in_=vbs[:, :, 64:66], func=AF.Copy), [b_v], [b_vl])
                        elif n == 512:
                            g4 = c0 // 128
                            T.op("dve", lambda v: v.tensor_tensor(out=vbp[:, g4:g4 + 4, 2:130],
                                                                  in0=pb.ap[:, 0:512].rearrange("p (a b) -> p a b", b=128),
                                                                  in1=xin_t[:, c0:c0 + 512].rearrange("p (a b) -> p a b", b=128),
                                                                  op=ALU.mult), [pb.buf, b_xin, b_cu], [b_v])
                        else:
                            T.op("dve", lambda v: v.tensor_tensor(out=vbs[:, :, 2:66],
                                                                  in0=pb.ap[:, 0:256].rearrange("p (a b) -> p a b", b=64),
                                                                  in1=xin_t[:, c0:c0 + 256].rearrange("p (a b) -> p a b", b=64),
                                                                  op=ALU.mult), [pb.buf, b_xin, b_cu], [b_v])
                    elif kind == "gb":
                        if not halo:
                            T.op("dve", lambda v: v.tensor_tensor(out=QX[:, j, c0:c0 + n], in0=pb.ap[:, 0:n],
                                                                  in1=cu[:, c0:c0 + n], op=ALU.mult),
                                 [pb.buf, b_cu], qx_bufs(gi))
                    else:
                        if not halo:
                            copy_op("act", QX[:, 12 + j, c0:c0 + n], pb.ap[:, 0:n], [pb.buf], qx_bufs(gi))
                linear_fm(ws, xT, src_bufs, w_in_a, cols, groups, cons)
                if save_state:
                    for j in range(12):
                        T.dma("sp", out=conv_out[:, j * 128:(j + 1) * 128].rearrange("r p -> p r"), in_=vlast[:, j, :],
                              reads=[b_vl], allow_slow_non_contiguous=True)
                T.barrier()

        def kv_project(tiles, dsts, ln_idx, do_inb):
            with ExitStack() as ph:
                ws = make_ws(ph, 4)
                stg = [(sb("kv_stg%d" % i, [128, 256], F32, ph), Buf()) for i in range(4)]
                c = {"n": 0}
                segs = [(ti * 128, 128, [B_xT[ti]]) for ti in tiles]
                ws.plan(tm_pieces(w_kv, 16, [i * 256 for i in range(12)]))
                ws._pump()
                ln_tiles(ln_idx, tiles, False, ph, nxb=1)

                def cons(bi, si, pb):
                    k = c["n"] % 4
                    c["n"] += 1
                    copy_op(evac_eng(), stg[k][0][:], pb.ap[:, 0:256], [pb.buf], [stg[k][1]])
                    T.dma("sp", out=dsts[si][:, bi * 256:(bi + 1) * 256], in_=stg[k][0][:], reads=[stg[k][1]])
                linear_tm(ws, xT, w_kv, 16, [i * 256 for i in range(12)], segs, cons, preplanned=True)
                if do_inb:
                    groups = [(0, 512, None), (512, 512, None), (1024, 256, None)]

                    def cons_b(ci, gi, pb):
                        c0, n, _ = groups[gi]
                        copy_op(evac_eng(), QX[:, ci, c0:c0 + n], pb.ap[:, 0:n], [pb.buf],
                                [B_qx[t] for t in range(c0 // 128, (c0 + n) // 128)])
                    linear_fm(ws, xT, lambda gi: [B_xT[t] for t in range(groups[gi][0] // 128, (groups[gi][0] + groups[gi][1]) // 128)],
                              w_in_b, [cc * 128 for cc in range(16)], groups, cons_b)
                T.barrier()

        B_kvd = Buf()
        B_memd = Buf()

        def dump_qx(name):
            if not DEBUG:
                return
            d = nc.dram_tensor("dbg_" + name, [128, 16, NTOK], F32, kind="ExternalOutput").ap()
            T.dma("pool", out=d, in_=QX[:], reads=B_qx)
            T.barrier()

        def dump_res(name):
            if not DEBUG:
                return
            d = nc.dram_tensor("dbg_" + name, [128, NT, D], F32, kind="ExternalOutput").ap()
            T.dma("sp", out=d, in_=resid[:], reads=B_res)
            T.barrier()

        def mem_kv_phase():
            with ExitStack() as ph:
                ws = make_ws(ph, 4)
                mst = sb("mst", [128, D], F32, ph)
                memT = sb("memT", [128, 16, 256], BF16, ph)
                xb1 = (sb("mk_xbf", [128, D], BF16, ph), Buf())
                xb = [xb1, xb1]
                stg = [(sb("mk_stg%d" % i, [128, 256], F32, ph), Buf()) for i in range(4)]
                bm, bmt = Buf(), Buf()
                for mc in range(2):
                    T.dma("sp", out=mst[:], in_=mem_p[mc * 128:(mc + 1) * 128, :], writes=[bm])
                    make_xT_tile(xb, mst[:], [bm], memT, mc * 128, [bmt])
                c = {"n": 0}
                for l in range(2):
                    def cons(bi, si, pb, l=l):
                        k = c["n"] % 4
                        c["n"] += 1
                        copy_op(evac_eng(), stg[k][0][:], pb.ap[:, 0:256], [pb.buf], [stg[k][1]])
                        T.dma("sp", out=memkv_p[l, si * 128:(si + 1) * 128, bi * 256:(bi + 1) * 256], in_=stg[k][0][:],
                              reads=[stg[k][1]])
                    linear_tm(ws, memT, w_mem_kv[l], 16, [i * 256 for i in range(4)],
                              [(0, 128, [bmt]), (128, 128, [bmt])], cons)
                T.barrier()

        def mem_segs(layer, tiles):
            segs = []
            for ti in tiles:
                if ti < 8:
                    segs.append((ti * 128, 128, ti, "p", memkv_p[layer, :, 0:512], memkv_p[layer, :, 512:1024]))
            for ti in tiles:
                if ti >= 8:
                    for hf in range(2):
                        s = (ti - 8) * 2 + hf
                        segs.append((ti * 128 + hf * 64, 64, ti, "s%d" % s, cmk[layer, s], cmv[layer, s]))
            return segs

        def stick_breaking():
            with ExitStack() as ph:
                Kb = [(sb("Kb%d" % i, [128, DMAIN], BF16, ph), Buf()) for i in range(3)]
                Vb = [(sb("Vb%d" % i, [128, DMAIN], BF16, ph), Buf()) for i in range(3)]
                KT = [(sb("KT%d" % i, [128, 12, 128], BF16, ph), Buf()) for i in range(2)]
                Eb = [(sb("Eb%d" % i, [128, 512], F32, ph), Buf()) for i in range(2)]
                Sb = [(sb("Sb%d" % i, [128, 512], F32, ph), Buf()) for i in range(2)]
                Xb = [(sb("Xb%d" % i, [128, 512], F32, ph), Buf()) for i in range(2)]
                Ab = [(sb("Ab%d" % i, [128, 512], BF16, ph), Buf()) for i in range(2)]
                SP = [(sb("SPacc%d" % i, [128, 512], F32, ph), Buf()) for i in range(3)]
                oacc = pbanks[0:3]
                rot = pbanks[3:6]
                rix = {"n": 0, "u": 0, "kb": 0}

                def rpsum():
                    b = rot[rix["n"] % 3]
                    rix["n"] += 1
                    return b

                def attend(c0, nq, ti, blocks, hpg=4):
                    N = hpg * nq
                    ng = 12 // hpg
                    nb = len(blocks)
                    nu = ng * nb
                    kbase = rix["kb"]
                    rix["kb"] += nb
                    for hg in range(ng):
                        T.op("dve", lambda v: v.memset(SP[hg][0][:, 0:N], 0.0), [], [SP[hg][1]])
                    st = {}

                    def load(bi):
                        k_d, v_d, nk, mask = blocks[bi]
                        kk = (kbase + bi) % 3
                        T.dma("pool", out=Kb[kk][0][0:nk, :], in_=k_d, reads=[B_kvd], writes=[Kb[kk][1]])
                        T.dma("pool", out=Vb[kk][0][0:nk, :], in_=v_d, reads=[B_kvd], writes=[Vb[kk][1]])

                    def transp(bi):
                        k_d, v_d, nk, mask = blocks[bi]
                        kk = (kbase + bi) % 3
                        kt = (kbase + bi) % 2
                        for half, (h0, hn) in enumerate(((0, 8), (8, 4))):
                            tb = tpsum()
                            T.group("pe", [
                                (lambda pe, hh=hh: pe.transpose(out=tb.ap[:, (hh - h0) * 128:(hh - h0) * 128 + nk],
                                                                in_=Kb[kk][0][0:nk, hh * 128:(hh + 1) * 128],
                                                                identity=identb[0:nk, 0:nk]))
                                for hh in range(h0, h0 + hn)], [Kb[kk][1], B_const], [tb.buf])
                            copy_op(evac_eng(), KT[kt][0][:, h0:h0 + hn, 0:nk],
                                    tb.ap[:, 0:hn * 128].rearrange("p (a b) -> p a b", b=128)[:, :, 0:nk],
                                    [tb.buf], [KT[kt][1]])

                    def stageA(u):
                        bi, hg = divmod(u, ng)
                        k_d, v_d, nk, mask = blocks[bi]
                        kt = (kbase + bi) % 2
                        E, S = Eb[u % 2], Sb[u % 2]
                        zb = rpsum()
                        T.group("pe", [
                            (lambda pe, hh=hh: pe.matmul(zb.ap[0:nk, hh * nq:(hh + 1) * nq],
                                                         lhsT=KT[kt][0][:, hg * hpg + hh, 0:nk],
                                                         rhs=QX[:, hg * hpg + hh, c0:c0 + nq], start=True, stop=True))
                            for hh in range(hpg)], [KT[kt][1], B_qx[ti]], [zb.buf])
                        T.op("act", lambda a: a.activation(out=E[0][0:nk, 0:N], in_=zb.ap[0:nk, 0:N], func=AF.Exp,
                                                           scale=SCALE), [zb.buf], [E[1]])
                        T.op("act", lambda a: a.activation(out=S[0][0:nk, 0:N], in_=E[0][0:nk, 0:N], func=AF.Ln,
                                                           bias=1.0), [E[1]], [S[1]])
                        if mask is not None:
                            for hh in range(hpg):
                                T.op("dve", lambda v, hh=hh: v.tensor_tensor(out=S[0][0:nk, hh * nq:(hh + 1) * nq],
                                                                             in0=S[0][0:nk, hh * nq:(hh + 1) * nq],
                                                                             in1=mask[0:nk, 0:nq], op=ALU.mult),
                                     [S[1], B_const], [S[1]])

                    def stageB(u):
                        bi, hg = divmod(u, ng)
                        k_d, v_d, nk, mask = blocks[bi]
                        E, S, X, A = Eb[u % 2], Sb[u % 2], Xb[u % 2], Ab[u % 2]
                        cb = rpsum()
                        fns = [lambda pe: pe.matmul(cb.ap[0:nk, 0:N], lhsT=tri[0:nk, 0:nk], rhs=S[0][0:nk, 0:N],
                                                    start=True, stop=(bi == 0))]
                        rds = [S[1], B_const]
                        if bi > 0:
                            fns.append(lambda pe: pe.matmul(cb.ap[0:nk, 0:N], lhsT=ones[:, 0:nk], rhs=SP[hg][0][:, 0:N],
                                                            start=False, stop=True))
                            rds.append(SP[hg][1])
                        T.group("pe", fns, rds, [cb.buf])
                        T.op("act", lambda a: a.activation(out=X[0][0:nk, 0:N], in_=cb.ap[0:nk, 0:N], func=AF.Exp,
                                                           scale=-1.0), [cb.buf], [X[1]])
                        T.op("dve", lambda v: v.tensor_tensor(out=A[0][0:nk, 0:N], in0=E[0][0:nk, 0:N],
                                                              in1=X[0][0:nk, 0:N], op=ALU.mult), [E[1], X[1]], [A[1]])
                        if mask is not None:
                            for hh in range(hpg):
                                T.op("dve", lambda v, hh=hh: v.tensor_tensor(out=A[0][0:nk, hh * nq:(hh + 1) * nq],
                                                                             in0=A[0][0:nk, hh * nq:(hh + 1) * nq],
                                                                             in1=mask[0:nk, 0:nq], op=ALU.mult),
                                     [A[1], B_const], [A[1]])
                        if bi < nb - 1:
                            T.op("dve", lambda v: v.tensor_tensor(out=SP[hg][0][0:nk, 0:N], in0=SP[hg][0][0:nk, 0:N],
                                                                  in1=S[0][0:nk, 0:N], op=ALU.add),
                                 [S[1], SP[hg][1]], [SP[hg][1]])

                    def stageC(u):
                        bi, hg = divmod(u, ng)
                        k_d, v_d, nk, mask = blocks[bi]
                        kk = (kbase + bi) % 3
                        A = Ab[u % 2]
                        T.group("pe", [
                            (lambda pe, hh=hh: pe.matmul(oacc[hg].ap[:, hh * nq:(hh + 1) * nq],
                                                         lhsT=Vb[kk][0][0:nk, (hg * hpg + hh) * 128:(hg * hpg + hh + 1) * 128],
                                                         rhs=A[0][0:nk, hh * nq:(hh + 1) * nq],
                                                         start=(bi == 0 and hh == 0), stop=(bi == nb - 1),
                                                         skip_group_check=True))
                            for hh in range(hpg)], [Vb[kk][1], A[1]], [oacc[hg].buf])

                    load(0)
                    transp(0)
                    if nb > 1:
                        load(1)
                    for t in range(nu + 2):
                        if t < nu:
                            stageA(t)
                            bi, hg = divmod(t, ng)
                            if hg == 1 and bi + 1 < nb:
                                transp(bi + 1)
                        if 0 <= t - 1 < nu:
                            stageB(t - 1)
                        if 0 <= t - 2 < nu:
                            stageC(t - 2)
                        if t < nu:
                            bi, hg = divmod(t, ng)
                            if hg == 1 and bi + 2 < nb:
                                load(bi + 2)
                    for hg in range(ng):
                        copy_op(evac_eng(), QX[:, hg * hpg:(hg + 1) * hpg, c0:c0 + nq],
                                oacc[hg].ap[:, 0:N].rearrange("p (a b) -> p a b", b=nq), [oacc[hg].buf], [B_qx[ti]])

                for s in range(4):
                    ti = 8 + s // 2
                    r0 = (s % 2) * 64
                    blocks = [(kv_smp[s // 2, r0:r0 + 64, 0:DMAIN], kv_smp[s // 2, r0:r0 + 64, DMAIN:3072], 64, mtri)]
                    for kb in range(31, -1, -1):
                        blocks.append((cache_k[s, kb * 128:(kb + 1) * 128, :], cache_v[s, kb * 128:(kb + 1) * 128, :], 128, None))
                    attend(ti * 128 + r0, 64, ti, blocks, hpg=6)
                for j in range(8):
                    blocks = []
                    for jj in range(j, -1, -1):
                        blocks.append((kv_own[jj, :, 0:DMAIN], kv_own[jj, :, DMAIN:3072], 128, mtri if jj == j else None))
                        blocks.append((kv_oth[jj, :, 0:DMAIN], kv_oth[jj, :, DMAIN:3072], 128, m0 if jj == 0 else None))
                    attend(j * 128, 128, j, blocks)
                T.barrier()

        def in_b_project(tiles):
            ntok = len(tiles) * 128
            with ExitStack() as ph:
                ws = make_ws(ph)
                groups = [(0, 512, None), (512, 512, None), (1024, 256, None)]

                def cons(ci, gi, pb):
                    c0, n, _ = groups[gi]
                    copy_op(evac_eng(), QX[:, ci, c0:c0 + n], pb.ap[:, 0:n], [pb.buf],
                            [B_qx[t] for t in range(c0 // 128, (c0 + n) // 128)])
                linear_fm(ws, xT, lambda gi: [B_xT[t] for t in range(groups[gi][0] // 128, (groups[gi][0] + groups[gi][1]) // 128)],
                          w_in_b, [c * 128 for c in range(16)], groups, cons)
                T.barrier()

        mem_kv_phase()
        P8 = list(range(8))
        P10 = list(range(10))
        load_tokens([x_oth[i] for i in range(8)], P8, h_oth)
        layer0_front(P8, False, False)
        mem_attention(0, mem_segs(0, P8))
        out_ln_ffn(0, P8, do_ln2=False)
        kv_project(P8, [kv_oth[i] for i in range(8)], 1, False)
        load_tokens([x_own[i] for i in range(8)] + [x_smp[i] for i in range(2)], P10, h_own)
        layer0_front(P10, True, True)
        mem_attention(0, mem_segs(0, P10))
        dump_qx("mixin0")
        out_ln_ffn(0, P10, "l0", do_ln2=False)
        kv_project(P10, [kv_own[i] for i in range(8)] + [kv_smp[i] for i in range(2)], 1, True)
        dump_qx("q")
        stick_breaking()
        mem_attention(1, mem_segs(1, P10))
        dump_qx("mixin1")
        out_ln_ffn(1, P10)
        for ti in P10:
            T.dma("sp", out=y_out[ti], in_=resid[:, ti, :], reads=[B_res[ti]])
        T.barrier()
    return nc


_CACHE = {}


def kernel(x_prompt, x_sample, state_conv, cache_k, cache_v, cache_mem_k, cache_mem_v, mem_prompt,
           w_in_a, conv_w, w_in_b, w_kv, w_mem_kv, w_out, w_ff1, w_ff2, ln_g, ln_b):
    f = lambda a: np.ascontiguousarray(np.asarray(a, dtype=np.float32))
    x_prompt, x_sample, state_conv = f(x_prompt), f(x_sample), f(state_conv)
    cache_k, cache_v, cache_mem_k, cache_mem_v, mem_prompt = f(cache_k), f(cache_v), f(cache_mem_k), f(cache_mem_v), f(mem_prompt)
    if "nc" not in _CACHE:
        _CACHE["nc"] = build_program()
    nc = _CACHE["nc"]
    ident = np.eye(128, dtype=np.float32)
    jj, ss = np.meshgrid(np.arange(128), np.arange(128), indexing="ij")
    tri = (jj >= ss).astype(np.float32)
    mtri = (jj < ss).astype(np.float32)
    ones = np.ones((128, 128), np.float32)
    shared = {
        "w_in_a": f(w_in_a)[0], "conv_w": f(conv_w)[0], "w_in_b": f(w_in_b)[0], "w_kv": f(w_kv),
        "w_mem_kv": f(w_mem_kv), "w_out": f(w_out), "w_ff1": f(w_ff1), "w_ff2": f(w_ff2),
        "ln_g": f(ln_g).reshape(4, D), "ln_b": f(ln_b).reshape(4, D),
        "c_ident": ident, "c_tri": tri, "c_ones": ones, "c_mtri": mtri,
    }
    xp = x_prompt.reshape(4, 16, 128, D)
    zeros2 = np.zeros((2, D), np.float32)

    def halo(b, g):
        return zeros2 if g == 0 else x_prompt[b, g * 128 - 2:g * 128]

    in_maps = []
    own_blocks, oth_blocks = [], []
    for c in range(NCORES):
        b, h = c // 2, c % 2
        own = [2 * j + h for j in range(8)]
        oth = [2 * j for j in range(8)] if h == 1 else [15] + [2 * j - 1 for j in range(1, 8)]
        own_blocks.append(own)
        oth_blocks.append(oth)
        m = dict(shared)
        m["x_own"] = np.ascontiguousarray(xp[b, own])
        m["x_oth"] = np.ascontiguousarray(xp[b, oth])
        m["x_smp"] = np.ascontiguousarray(x_sample[4 * c:4 * c + 4].reshape(2, 128, D))
        m["h_own"] = np.ascontiguousarray(np.concatenate([halo(b, g) for g in own], axis=0))
        m["h_oth"] = np.ascontiguousarray(np.concatenate([halo(b, g) for g in oth], axis=0))
        m["sconv"] = np.ascontiguousarray(state_conv[0, 4 * c:4 * c + 4].reshape(8, DMAIN))
        m["cache_k"] = np.ascontiguousarray(cache_k[4 * c:4 * c + 4].reshape(4, 4096, DMAIN))
        m["cache_v"] = np.ascontiguousarray(cache_v[4 * c:4 * c + 4].reshape(4, 4096, DMAIN))
        m["cmk"] = np.ascontiguousarray(cache_mem_k[:, 4 * c:4 * c + 4].reshape(2, 4, NMEM, DMEM))
        m["cmv"] = np.ascontiguousarray(cache_mem_v[:, 4 * c:4 * c + 4].reshape(2, 4, NMEM, DMEM))
        m["mem_p"] = np.ascontiguousarray(mem_prompt[b])
        m["c_m0"] = ones if h == 1 else np.zeros((128, 128), np.float32)
        in_maps.append(m)
    res = run_bass_kernel_spmd(nc, in_maps, core_ids=list(range(NCORES)))
    R = res.results
    if DEBUG:
        _CACHE["dbg"] = {k: np.asarray(v) for k, v in R[0].items() if k.startswith("dbg_") or k in ("kv_own", "kv_smp", "kv_oth", "memkv_p", "y_out")}
    y_prompt = np.zeros((4, 16, 128, D), np.float32)
    y_sample = np.zeros((32, 64, D), np.float32)
    conv_p = np.zeros((1, 4, 2, DMAIN), np.float32)
    conv_s = np.zeros((1, 32, 2, DMAIN), np.float32)
    kvp = np.zeros((4, 16, 128, 3072), np.float32)
    kvs = np.zeros((32, 64, 3072), np.float32)
    mkp = np.zeros((2, 4, NMEM, DMEM), np.float32)
    mvp = np.zeros((2, 4, NMEM, DMEM), np.float32)
    for c in range(NCORES):
        b, h = c // 2, c % 2
        r = R[c]
        y = np.asarray(r["y_out"])
        for j, g in enumerate(own_blocks[c]):
            y_prompt[b, g] = y[j]
            kvp[b, g] = np.asarray(r["kv_own"])[j]
        y_sample[4 * c:4 * c + 4] = y[8:10].reshape(4, 64, D)
        kvs[4 * c:4 * c + 4] = np.asarray(r["kv_smp"]).reshape(4, 64, 3072)
        co = np.asarray(r["conv_out"])
        conv_s[0, 4 * c:4 * c + 4] = co[2:10].reshape(4, 2, DMAIN)
        if h == 1:
            conv_p[0, b] = co[0:2]
            mk = np.asarray(r["memkv_p"])
            mkp[:, b] = mk[:, :, 0:512]
            mvp[:, b] = mk[:, :, 512:1024]
    y_prompt = y_prompt.reshape(4, 2048, D)
    kvp = kvp.reshape(4, 2048, 3072)
    k_p = np.ascontiguousarray(kvp[:, :, 0:DMAIN]).reshape(4, 2048, 12, 128)
    v_p = np.ascontiguousarray(kvp[:, :, DMAIN:]).reshape(4, 2048, 12, 128)
    k_s = np.ascontiguousarray(kvs[:, :, 0:DMAIN]).reshape(32, 64, 12, 128)
    v_s = np.ascontiguousarray(kvs[:, :, DMAIN:]).reshape(32, 64, 12, 128)
    return (y_prompt, y_sample, conv_p, conv_s, k_p, v_p, k_s, v_s,
            mkp.reshape(2, 4, NMEM, 4, 128), mvp.reshape(2, 4, NMEM, 4, 128))
```
